# Optimizing a Trainium2 kernel written in Bass

```python
import jax, jax.numpy as jnp
from jax import lax
import numpy as np

D_MODEL = 1024
BATCH = 2
SEQ = 16384
DEPTH = 2

N_MIXERS = 2
N_HEADS = 16
HEAD_DIM = D_MODEL // N_HEADS
MOBA_BLOCK = 256
MOBA_TOPK = 3
Q_CHUNK = 32
ROPE_THETA = 10000.0
CONV_WIDTH = 3
D_FF = 2816
RMS_EPS = 1e-6
NEG_INF = -1e30
N_CONV_LAYERS = (DEPTH + 1) // 2
N_ATTN_LAYERS = DEPTH // 2

kernel_name = "hybrid_shortconv_moba_convffn"


def rmsnorm(x, g):
    xf = x.astype(jnp.float32)
    var = jnp.mean(xf * xf, axis=-1, keepdims=True)
    return (xf * lax.rsqrt(var + RMS_EPS)).astype(x.dtype) * g


def causal_dwconv(x, w):
    S = x.shape[1]
    xp = jnp.pad(x, ((0, 0), (CONV_WIDTH - 1, 0), (0, 0)))
    y = xp[:, 0:S] * w[0]
    for j in range(1, CONV_WIDTH):
        y = y + xp[:, j:j + S] * w[j]
    return y


def short_conv_mixer(h, w_in, w_conv, w_out):
    bcv = h @ w_in
    b, c, v = jnp.split(bcv, 3, axis=-1)
    return (b * causal_dwconv(c * v, w_conv)) @ w_out


def rope(x, pos):
    half = HEAD_DIM // 2
    inv = ROPE_THETA ** (-jnp.arange(half, dtype=jnp.float32) / half)
    ang = pos.astype(jnp.float32)[:, None] * inv[None, :]
    cos = jnp.cos(ang).astype(x.dtype)
    sin = jnp.sin(ang).astype(x.dtype)
    x1, x2 = x[..., :half], x[..., half:]
    return jnp.concatenate([x1 * cos - x2 * sin, x2 * cos + x1 * sin], axis=-1)


def moba_attention(h, w_qkv, w_o):
    Bsz, S, _ = h.shape
    qkv = (h @ w_qkv).reshape(Bsz, S, 3, N_HEADS, HEAD_DIM)
    q = jnp.transpose(qkv[:, :, 0], (0, 2, 1, 3))
    k = jnp.transpose(qkv[:, :, 1], (0, 2, 1, 3))
    v = jnp.transpose(qkv[:, :, 2], (0, 2, 1, 3))
    pos = jnp.arange(S, dtype=jnp.int32)
    q = rope(q, pos) * (HEAD_DIM ** -0.5)
    k = rope(k, pos)

    n_blocks = -(-S // MOBA_BLOCK)
    pad = n_blocks * MOBA_BLOCK - S
    k = jnp.pad(k, ((0, 0), (0, 0), (0, pad), (0, 0)))
    v = jnp.pad(v, ((0, 0), (0, 0), (0, pad), (0, 0)))
    k_blocks = k.reshape(Bsz, N_HEADS, n_blocks, MOBA_BLOCK, HEAD_DIM)
    v_blocks = v.reshape(Bsz, N_HEADS, n_blocks, MOBA_BLOCK, HEAD_DIM)
    k_mean = jnp.mean(k_blocks.astype(jnp.float32), axis=3).astype(k.dtype)

    topk = min(MOBA_TOPK, n_blocks)
    b_idx = jnp.arange(Bsz)[:, None, None, None]
    h_idx = jnp.arange(N_HEADS)[None, :, None, None]
    block_ids = jnp.arange(n_blocks)
    key_off = jnp.arange(MOBA_BLOCK)
    q_off = jnp.arange(Q_CHUNK)

    def chunk(c):
        q0 = c * Q_CHUNK
        qc = lax.dynamic_slice_in_dim(q, q0, Q_CHUNK, axis=2)
        qpos = q0 + q_off
        own = q0 // MOBA_BLOCK
        gate = jnp.einsum('bhqd,bhnd->bhqn', qc, k_mean).astype(jnp.float32)
        gate = jnp.where((block_ids < own)[None, None, None, :], gate, NEG_INF)
        _, top_i = lax.top_k(gate, topk)
        rank_valid = jnp.arange(topk) < jnp.minimum(own, MOBA_TOPK)
        kg = k_blocks[b_idx, h_idx, top_i]
        vg = v_blocks[b_idx, h_idx, top_i]
        s_sel = jnp.einsum('bhqd,bhqnjd->bhqnj', qc, kg).astype(jnp.float32)
        s_sel = jnp.where(rank_valid[:, None], s_sel, NEG_INF)
        s_sel = s_sel.reshape(Bsz, N_HEADS, Q_CHUNK, topk * MOBA_BLOCK)
        ko = lax.dynamic_slice_in_dim(k, own * MOBA_BLOCK, MOBA_BLOCK, axis=2)
        vo = lax.dynamic_slice_in_dim(v, own * MOBA_BLOCK, MOBA_BLOCK, axis=2)
        s_own = jnp.einsum('bhqd,bhjd->bhqj', qc, ko).astype(jnp.float32)
        kpos = own * MOBA_BLOCK + key_off
        s_own = jnp.where(kpos[None, :] <= qpos[:, None], s_own, NEG_INF)
        p = jax.nn.softmax(jnp.concatenate([s_sel, s_own], axis=-1), axis=-1).astype(v.dtype)
        p_sel = p[..., :topk * MOBA_BLOCK].reshape(Bsz, N_HEADS, Q_CHUNK, topk, MOBA_BLOCK)
        p_own = p[..., topk * MOBA_BLOCK:]
        return (jnp.einsum('bhqnj,bhqnjd->bhqd', p_sel, vg)
                + jnp.einsum('bhqj,bhjd->bhqd', p_own, vo))

    o = lax.map(chunk, jnp.arange(S // Q_CHUNK))
    o = jnp.transpose(o, (1, 0, 3, 2, 4)).reshape(Bsz, S, N_HEADS * HEAD_DIM)
    return o @ w_o


def conv_ffn(h, w_up, w_conv, w_down):
    gu = causal_dwconv(h @ w_up, w_conv)
    g, u = jnp.split(gu, 2, axis=-1)
    return (jax.nn.silu(g) * u) @ w_down


def setup_inputs(seed: int = 0) -> dict:
    key = jax.random.key(seed)
    ks = jax.random.split(key, 16)
    D = D_MODEL
    f32 = jnp.float32

    def nrm(k, shape, scale):
        return jax.random.normal(k, shape, dtype=f32) * scale

    return {
        "x": nrm(ks[0], (BATCH, SEQ, D), 1.0),
        "mix_norm": 1.0 + nrm(ks[1], (DEPTH, D), 0.02),
        "sc_w_in": nrm(ks[2], (N_CONV_LAYERS, D, 3 * D), D ** -0.5),
        "sc_w_conv": nrm(ks[3], (N_CONV_LAYERS, CONV_WIDTH, D), CONV_WIDTH ** -0.5),
        "sc_w_out": nrm(ks[4], (N_CONV_LAYERS, D, D), D ** -0.5),
        "moba_w_qkv": nrm(ks[5], (N_ATTN_LAYERS, D, 3 * D), D ** -0.5),
        "moba_w_o": nrm(ks[6], (N_ATTN_LAYERS, D, D), D ** -0.5),
        "ffn_norm": 1.0 + nrm(ks[7], (DEPTH, D), 0.02),
        "ffn_w_up": nrm(ks[8], (DEPTH, D, 2 * D_FF), D ** -0.5),
        "ffn_w_conv": nrm(ks[9], (DEPTH, CONV_WIDTH, 2 * D_FF), CONV_WIDTH ** -0.5),
        "ffn_w_down": nrm(ks[10], (DEPTH, D_FF, D), D_FF ** -0.5),
        "final_norm": 1.0 + nrm(ks[11], (D,), 0.02),
    }


def reference(x, mix_norm, sc_w_in, sc_w_conv, sc_w_out, moba_w_qkv, moba_w_o,
              ffn_norm, ffn_w_up, ffn_w_conv, ffn_w_down, final_norm):
    for i in range(DEPTH):
        h = rmsnorm(x, mix_norm[i])
        j = i // N_MIXERS
        if i % N_MIXERS == 0:
            x = x + short_conv_mixer(h, sc_w_in[j], sc_w_conv[j], sc_w_out[j])
        else:
            x = x + moba_attention(h, moba_w_qkv[j], moba_w_o[j])
        x = x + conv_ffn(rmsnorm(x, ffn_norm[i]), ffn_w_up[i], ffn_w_conv[i], ffn_w_down[i])
    return rmsnorm(x, final_norm)
```

```python
import numpy as np
from contextlib import ExitStack
import concourse.bass as bass
import concourse.mybir as mybir
from concourse.bass_utils import run_bass_kernel_spmd
import ml_dtypes

F32 = mybir.dt.float32
BF16 = mybir.dt.bfloat16
AF = mybir.ActivationFunctionType
ALU = mybir.AluOpType

D = 1024
DFF = 2816
NH = 16
HD = 64
SEQ = 16384
BATCH = 2
NCORE = 8
TOK = 4096
HALO = 6
NT = TOK + HALO
TILES = [(0, HALO)] + [(HALO + 512 * i, 512) for i in range(8)]
EPS = 1e-6
NEG = -30000.0
ENGS = ["pe", "act", "dve", "pool", "sp"]


class Sched:
    def __init__(self, nc, stack):
        self.nc = nc
        self.stack = stack
        self.ops = {e: [] for e in ENGS}
        self.esem = {e: stack.enter_context(nc.semaphore("se_" + e)) for e in ENGS}
        self.ecount = {e: 0 for e in ENGS}
        self.epoch = 0
        self.waited = {}
        self.lastw = {}
        self.readers = {}
        self.dsem = {}

    def _dstream(self, name):
        if name not in self.dsem:
            sem = self.stack.enter_context(self.nc.semaphore("sd_%d" % len(self.dsem)))
            self.dsem[name] = [sem, 0]
        return self.dsem[name]

    def op(self, eng, fn, reads=(), writes=(), dma=None, multi=False):
        deps = []
        for b in reads:
            t = self.lastw.get(b)
            if t is not None:
                deps.append(t)
            if isinstance(b, tuple) and b[0] == "ps":
                deps.extend(r for r in self.readers.get(b, ()) if r[3] != eng)
        for b in writes:
            t = self.lastw.get(b)
            if t is not None:
                deps.append(t)
            deps.extend(self.readers.get(b, ()))
        need = {}
        for (key, sem, val, src) in deps:
            if eng == "pe" and src == "pe":
                continue
            if key not in need or need[key][1] < val:
                need[key] = (sem, val)
        waits = []
        for key, (sem, val) in need.items():
            k = (eng, key)
            if self.waited.get(k, 0) >= val:
                continue
            self.waited[k] = val
            waits.append((sem, val))
        if dma is not None:
            ent = self._dstream(dma)
            ent[1] += 16
            tok = (("d", dma), ent[0], ent[1], "dma")
            inc = 16
        else:
            self.ecount[eng] += 1
            tok = (("e", eng, self.epoch), self.esem[eng], self.ecount[eng], eng)
            inc = 1
        self.ops[eng].append((waits if not multi else [("multi",)] + waits, fn, tok[1], inc))
        for b in reads:
            self.readers.setdefault(b, []).append(tok)
        for b in writes:
            self.lastw[b] = tok
            self.readers[b] = []
        return tok

    def barrier(self, scratch_ap):
        final = [(ent[0], ent[1]) for ent in self.dsem.values() if ent[1] > 0]
        final += [(self.esem[en], self.ecount[en]) for en in ENGS if en != "sp" and self.ecount[en] > 0]
        ent = self._dstream("__barrier__")
        ent[1] += 16
        dst, src = scratch_ap
        self.ops["sp"].append((final, (lambda e: e.dma_start(out=dst, in_=src)), ent[0], 16))
        for en in ENGS:
            self.ops[en].append(([(ent[0], ent[1])], None, None, 0))
        self.lastw.clear()
        self.readers.clear()

    def new_epoch(self):
        self.epoch += 1
        self.esem = {e: self.stack.enter_context(self.nc.semaphore("se%d_%s" % (self.epoch, e))) for e in ENGS}
        self.ecount = {e: 0 for e in ENGS}
        self.waited = {}

    def emit(self, last=True):
        nc = self.nc
        final = [(ent[0], ent[1]) for ent in self.dsem.values() if ent[1] > 0]

        def replay(lst, e, last=False, attach=True):
            for waits, fn, sem, inc in lst:
                multi = bool(waits) and waits[0] == ("multi",)
                if multi:
                    waits = waits[1:]
                if fn is None or not attach or not waits or multi:
                    for (s, v) in waits:
                        e.wait_ge(s, v)
                    if fn is not None:
                        fn(e).then_inc(sem, inc)
                else:
                    for (s, v) in waits[:-1]:
                        e.wait_ge(s, v)
                    ins = fn(e)
                    ins._wait_ge(*waits[-1])
                    ins.then_inc(sem, inc)
            if last:
                for (s, v) in final:
                    e.wait_ge(s, v)
                for en in ENGS:
                    if self.ecount[en] > 0:
                        e.wait_ge(self.esem[en], self.ecount[en])

        with nc.Block() as block:
            @block.tensor
            def _(e):
                replay(self.ops["pe"], e, attach=False)

            @block.scalar
            def _(e):
                replay(self.ops["act"], e)

            @block.vector
            def _(e):
                replay(self.ops["dve"], e)

            @block.gpsimd
            def _(e):
                replay(self.ops["pool"], e)

            @block.sync
            def _(e):
                replay(self.ops["sp"], e, last=last)
        self.ops = {e: [] for e in ENGS}


class Prog:
    RING = 4
    SLOT = 6144

    def __init__(self):
        self.nc = bass.Bass("TRN2", target_bir_lowering=False)
        self.stack = ExitStack()
        self.S = Sched(self.nc, self.stack)
        self.ps = [self.stack.enter_context(self.nc.psum_tensor("ps%d" % i, [128, 512], F32)) for i in range(8)]
        self.ps_rr = {}
        self.blocks = []
        self.issued = 0
        self.cons = 0
        self.released = 0
        self.slots = None
        self.uid = 0
        self.pstack = None
        self.phase_id = 0
        self.use_wscr = False
        self.wscr = {}
        bar = self.nc.dram_tensor("bar_scratch", [2, 64], F32, kind="Internal").ap()
        self.bar_dst, self.bar_src = bar[0:1, :], bar[1:2, :]

    def dram(self, name, shape, dtype, kind):
        return self.nc.dram_tensor(name, list(shape), dtype, kind=kind).ap()

    def sb(self, name, shape, dtype):
        st = self.pstack if self.pstack is not None else self.stack
        self.uid += 1
        return st.enter_context(self.nc.sbuf_tensor("%s_u%d" % (name, self.uid), list(shape), dtype))

    def phase_begin(self):
        self.phase_id += 1
        self.pstack = ExitStack()
        self.blocks = []
        self.issued = self.cons = self.released = 0
        self.ps_rr = {}

    def phase_end(self, last=False):
        if not last:
            self.S.barrier((self.bar_dst, self.bar_src))
        self.S.emit(last=last)
        if not last:
            self.S.new_epoch()
        self.pstack.close()
        self.pstack = None

    def mid_barrier(self):
        self.S.barrier((self.bar_dst, self.bar_src))
        self.S.emit(last=False)
        self.S.new_epoch()

    def bank(self, pool, banks):
        i = self.ps_rr.get(pool, 0)
        self.ps_rr[pool] = i + 1
        return banks[i % len(banks)]

    def op(self, *a, **k):
        return self.S.op(*a, **k)

    def ring_init(self):
        self.slots = [self.sb("wslot%d" % i, [128, self.SLOT], BF16) for i in range(self.RING)]

    def ring_plan(self, blocks):
        self.blocks.extend(blocks)

    def _issue(self, i):
        tag, segs = self.blocks[i]
        s = i % self.RING
        slot = self.slots[s]
        key = (self.phase_id, tag)
        total = segs[0][1] * segs[0][3]
        names = [("w", s, si) for si in range(len(segs))]
        if self.use_wscr and key in self.wscr:
            scr = self.wscr[key]
            self.op("pool", (lambda e: e.dma_start(out=slot[:, 0:total], in_=scr[:, 0:total])),
                    reads=[("wscr", key)], writes=names, dma="w%d_0" % s)
            return
        for si, (off, kc, n, stride, src) in enumerate(segs):
            dst = slot[:, 0:kc * stride].rearrange("p (k n) -> p k n", n=stride)[:, :, off:off + n]
            self.op("pool", (lambda e, dst=dst, src=src: e.dma_start(out=dst, in_=src)),
                    writes=[("w", s, si)], dma="w%d_%d" % (s, si))
        if self.use_wscr:
            scr = self.dram("wscr_p%d_%s" % key, [128, self.SLOT], BF16, "Internal")
            self.wscr[key] = scr
            self.op("pool", (lambda e: e.dma_start(out=scr[:, 0:total], in_=slot[:, 0:total])),
                    reads=names, writes=[("wscr", key)], dma="ws%d" % s)

    def _pump(self):
        while self.issued < len(self.blocks) and self.issued < self.released + self.RING:
            self._issue(self.issued)
            self.issued += 1

    def ring_next(self, tag):
        i = self.cons
        self.cons += 1
        assert self.blocks[i][0] == tag, (self.blocks[i][0], tag)
        self._pump()
        assert self.issued > i, "ring: too many live blocks"
        s = i % self.RING
        nseg = len(self.blocks[i][1])
        return self.slots[s], [("w", s, si) for si in range(nseg)]

    def ring_done(self, n=1):
        self.released += n
        self._pump()

    presq = False

    def consts(self):
        self.ones = self.sb("ones_bf", [128, 128], BF16)
        self.op("dve", lambda e: e.memset(self.ones[:], 1.0), writes=["ones"])
        if self.presq:
            self.sqn = [self.sb("sqn%d" % i, [128, 512], BF16) for i in range(8)]
            self.sq = self.sqn[0:4]
        else:
            self.sq = [self.sb("sq%d" % i, [128, 512], BF16) for i in range(4)]
        self.sq_i = 0
        self.rsb = self.sb("rsb", [128, 512], F32)
        self.rstd = self.sb("rstd", [128, 512], F32)
        self.rscr = self.sb("rscr", [128, 512], F32)

    def rmsnorm_stats(self, xt, xname, W, banks):
        bk = self.bank("A", banks)
        ps = self.ps[bk]
        for c in range(8):
            i = self.sq_i % 4
            self.sq_i += 1
            sq = self.sq[i]
            self.op("act", (lambda e, sq=sq, c=c: e.activation(out=sq[:, 0:W], in_=xt[:, c, 0:W], func=AF.Square)),
                    reads=[(xname, c)], writes=[("sq", i)])
            self.op("pe", (lambda e, sq=sq, c=c: e.matmul(ps[:, 0:W], lhsT=self.ones[:], rhs=sq[:, 0:W],
                                                          start=(c == 0), stop=(c == 7))),
                    reads=[("sq", i), "ones"], writes=[("ps", bk)])
        self.op("dve", (lambda e: e.tensor_scalar(out=self.rsb[:, 0:W], in0=ps[:, 0:W], scalar1=1.0 / D, scalar2=EPS,
                                                  op0=ALU.mult, op1=ALU.add)),
                reads=[("ps", bk)], writes=["rsb"])
        self.op("act", (lambda e: e.activation(out=self.rsb[:, 0:W], in_=self.rsb[:, 0:W], func=AF.Sqrt)),
                reads=["rsb"], writes=["rsb"])
        self.op("dve", (lambda e: e.reciprocal(out=self.rstd[:, 0:W], in_=self.rsb[:, 0:W])),
                reads=["rsb"], writes=["rstd"])

    def sq_emit(self, xt, xname, c, W):
        if not self.presq:
            return
        self.op("act", (lambda e: e.activation(out=self.sqn[c][:, 0:W], in_=xt[:, c, 0:W], func=AF.Square)),
                reads=[(xname, c)], writes=[("sqn", c)])

    def rmsnorm_stats_pre(self, W, banks):
        bk = self.bank("A", banks)
        ps = self.ps[bk]
        for c in range(8):
            self.op("pe", (lambda e, c=c: e.matmul(ps[:, 0:W], lhsT=self.ones[:], rhs=self.sqn[c][:, 0:W], start=(c == 0), stop=(c == 7))),
                    reads=[("sqn", c), "ones"], writes=[("ps", bk)])
        self.op("dve", (lambda e: e.tensor_scalar(out=self.rsb[:, 0:W], in0=ps[:, 0:W], scalar1=1.0 / D, scalar2=EPS,
                                                  op0=ALU.mult, op1=ALU.add)),
                reads=[("ps", bk)], writes=["rsb"])
        self.op("act", (lambda e: e.activation(out=self.rsb[:, 0:W], in_=self.rsb[:, 0:W], func=AF.Sqrt)),
                reads=["rsb"], writes=["rsb"])
        self.op("dve", (lambda e: e.reciprocal(out=self.rstd[:, 0:W], in_=self.rsb[:, 0:W])),
                reads=["rsb"], writes=["rstd"])

    def rmsnorm_apply(self, xt, xname, W, gcol, out, oname):
        for c in range(8):
            self.op("dve", (lambda e, c=c: e.scalar_tensor_tensor(out=out[:, c, 0:W], in0=xt[:, c, 0:W],
                                                                    scalar=gcol[:, c:c + 1], in1=self.rstd[:, 0:W],
                                                                    op0=ALU.mult, op1=ALU.mult)),
                    reads=[(xname, c), "rstd", "gains"], writes=[(oname, c)])

    def conv3(self, buf, bname, t, tname, W, wc, ci, hs, hsname):
        self.op("act", (lambda e: e.activation(out=buf[:, 0:2], in_=hs[:, ci, :], func=AF.Copy)),
                reads=[(hsname, ci)], writes=[(bname, "h")])
        self.op("act", (lambda e: e.activation(out=t[:, 0:W], in_=buf[:, 0:W], func=AF.Copy, scale=wc[:, 0, ci:ci + 1])),
                reads=[(bname, "h"), (bname, "m"), "wc"], writes=[tname])
        self.op("dve", (lambda e: e.scalar_tensor_tensor(out=t[:, 0:W], in0=buf[:, 1:1 + W], scalar=wc[:, 1, ci:ci + 1],
                                                         in1=t[:, 0:W], op0=ALU.mult, op1=ALU.add)),
                reads=[(bname, "h"), (bname, "m"), tname, "wc"], writes=[tname])
        self.op("dve", (lambda e: e.scalar_tensor_tensor(out=t[:, 0:W], in0=buf[:, 2:2 + W], scalar=wc[:, 2, ci:ci + 1],
                                                         in1=t[:, 0:W], op0=ALU.mult, op1=ALU.add)),
                reads=[(bname, "m"), tname, "wc"], writes=[tname])
        self.op("act", (lambda e: e.activation(out=hs[:, ci, :], in_=buf[:, W:W + 2], func=AF.Copy)),
                reads=[(bname, "m"), (bname, "h")], writes=[(hsname, ci)])

    def ffn_setup(self):
        self.actT = self.sb("actT", [128, 22, 512], BF16)
        self.gb = [self.sb("gb%d" % i, [128, 514], F32) for i in range(2)]
        self.ub = [self.sb("ub%d" % i, [128, 514], F32) for i in range(2)]
        self.tg = [self.sb("tg%d" % i, [128, 512], F32) for i in range(2)]
        self.tu = [self.sb("tu%d" % i, [128, 512], F32) for i in range(2)]
        self.hsf = self.sb("hsf", [128, 44, 2], F32)
        self.op("dve", lambda e: e.memset(self.hsf[:], 0.0), writes=[("hsf", i) for i in range(44)])
        self.pair_i = 0

    @staticmethod
    def ffn_blocks(w_up, w_down):
        bl = []
        for b in range(11):
            g = w_up[:, 256 * b:256 * b + 256].rearrange("(k p) n -> p k n", p=128)
            u = w_up[:, DFF + 256 * b:DFF + 256 * b + 256].rearrange("(k p) n -> p k n", p=128)
            bl.append(("up%d" % b, [(0, 8, 256, 512, g), (256, 8, 256, 512, u)]))
        for dh in range(2):
            for kh in range(2):
                src = w_down[kh * 1408:(kh + 1) * 1408, dh * 512:(dh + 1) * 512].rearrange("(k p) n -> p k n", p=128)
                bl.append(("dn%d%d" % (dh, kh), [(0, 11, 512, 512, src)]))
        return bl

    def ffn(self, xt, xname, hT, W, wcf, hook=None):
        pend = None
        for b in range(11):
            slot, wnames = self.ring_next("up%d" % b)
            for s in range(2):
                fc = 2 * b + s
                pi = self.pair_i % 2
                self.pair_i += 1
                bg = self.bank("A", [0, 1, 2, 3])
                bu = self.bank("A", [0, 1, 2, 3])
                for (bk, off) in ((bg, 0), (bu, 256)):
                    for kc in range(8):
                        lw = slot[:, kc * 512 + off + s * 128: kc * 512 + off + s * 128 + 128]
                        self.op("pe", (lambda e, bk=bk, lw=lw, kc=kc: e.matmul(self.ps[bk][:, 0:W], lhsT=lw, rhs=hT[:, kc, 0:W],
                                                                                start=(kc == 0), stop=(kc == 7))),
                                reads=wnames + [("hT", kc)], writes=[("ps", bk)])
                gb, ub, tg, tu = self.gb[pi], self.ub[pi], self.tg[pi], self.tu[pi]
                self.op("act", (lambda e, gb=gb, bg=bg: e.activation(out=gb[:, 2:2 + W], in_=self.ps[bg][:, 0:W], func=AF.Copy)),
                        reads=[("ps", bg)], writes=[("gb%d" % pi, "m")])
                self.op("act", (lambda e, ub=ub, bu=bu: e.activation(out=ub[:, 2:2 + W], in_=self.ps[bu][:, 0:W], func=AF.Copy)),
                        reads=[("ps", bu)], writes=[("ub%d" % pi, "m")])
                self.conv3(gb, "gb%d" % pi, tg, "tg%d" % pi, W, wcf, fc, self.hsf, "hsf")
                self.conv3(ub, "ub%d" % pi, tu, "tu%d" % pi, W, wcf, 22 + fc, self.hsf, "hsf")
                if pend is not None:
                    pend()
                if hook:
                    hook.pop(0)()

                def stage2(tg=tg, tu=tu, pi=pi, fc=fc):
                    self.op("act", (lambda e: e.activation(out=tg[:, 0:W], in_=tg[:, 0:W], func=AF.Silu)),
                            reads=["tg%d" % pi], writes=["tg%d" % pi])
                    self.op("dve", (lambda e: e.tensor_tensor(out=self.actT[:, fc, 0:W], in0=tg[:, 0:W], in1=tu[:, 0:W], op=ALU.mult)),
                            reads=["tg%d" % pi, "tu%d" % pi], writes=[("actT", fc)])
                pend = stage2
            self.ring_done()
        pend()
        while hook:
            hook.pop(0)()
        import os
        if os.environ.get("DBG_FFN") == "1":
            self.cons += 4
            self.released += 4
            return
        for dh in range(2):
            banks = [4, 5, 6, 7] if dh == 0 else [0, 1, 2, 3]
            sl = [self.ring_next("dn%d%d" % (dh, kh)) for kh in range(2)]
            for o in range(4):
                for kh in range(2):
                    slot, wnames = sl[kh]
                    for kc in range(11):
                        lw = slot[:, kc * 512 + o * 128: kc * 512 + o * 128 + 128]
                        fc = kh * 11 + kc
                        self.op("pe", (lambda e, lw=lw, o=o, fc=fc, kh=kh, kc=kc, banks=banks: e.matmul(
                            self.ps[banks[o]][:, 0:W], lhsT=lw, rhs=self.actT[:, fc, 0:W],
                            start=(kh == 0 and kc == 0), stop=(kh == 1 and kc == 10))),
                            reads=wnames + [("actT", fc)], writes=[("ps", banks[o])])
            self.ring_done(2)
            for o in range(4):
                c = dh * 4 + o
                self.op("dve", (lambda e, c=c, o=o, banks=banks: e.tensor_tensor(out=xt[:, c, 0:W], in0=xt[:, c, 0:W],
                                                                                 in1=self.ps[banks[o]][:, 0:W], op=ALU.add)),
                        reads=[(xname, c), ("ps", banks[o])], writes=[(xname, c)])
                self.sq_emit(xt, xname, c, W)

    @staticmethod
    def sq_blocks(w, tag):
        bl = []
        for ob in range(2):
            src = w[:, 512 * ob:512 * ob + 512].rearrange("(k p) n -> p k n", p=128)
            bl.append(("%s%d" % (tag, ob), [(0, 8, 512, 512, src)]))
        return bl

    def proj_add(self, xt, xname, rT, rname, W, tag, banks):
        for ob in range(2):
            slot, wnames = self.ring_next("%s%d" % (tag, ob))
            for o in range(4):
                bk = self.bank("B", banks)
                for kc in range(8):
                    lw = slot[:, kc * 512 + o * 128: kc * 512 + o * 128 + 128]
                    self.op("pe", (lambda e, bk=bk, lw=lw, kc=kc: e.matmul(self.ps[bk][:, 0:W], lhsT=lw, rhs=rT[:, kc, 0:W],
                                                                            start=(kc == 0), stop=(kc == 7))),
                            reads=wnames + [(rname, kc)], writes=[("ps", bk)])
                c = ob * 4 + o
                self.op("dve", (lambda e, c=c, bk=bk: e.tensor_tensor(out=xt[:, c, 0:W], in0=xt[:, c, 0:W],
                                                                       in1=self.ps[bk][:, 0:W], op=ALU.add)),
                        reads=[(xname, c), ("ps", bk)], writes=[(xname, c)])
                self.sq_emit(xt, xname, c, W)
            self.ring_done()


def build_A():
    P = Prog()
    nc = P.nc
    xT = P.dram("xT", [128, 8, NT], F32, "ExternalInput")
    pos = P.dram("pos", [1, TOK], F32, "ExternalInput")
    gains_d = P.dram("gains", [128, 3, 8], F32, "ExternalInput")
    wcm_d = P.dram("wcm", [128, 3, 8], F32, "ExternalInput")
    wcf_d = P.dram("wcf", [128, 3, 44], F32, "ExternalInput")
    w_in = P.dram("w_in", [D, 3 * D], F32, "ExternalInput")
    w_out = P.dram("w_out", [D, D], F32, "ExternalInput")
    w_up = P.dram("w_up", [D, 2 * DFF], F32, "ExternalInput")
    w_down = P.dram("w_down", [DFF, D], F32, "ExternalInput")
    w_qkv = P.dram("w_qkv", [D, 3 * D], F32, "ExternalInput")
    x1T = P.dram("x1T", [128, 8, NT], F32, "ExternalOutput")
    QT = P.dram("QT", [D, TOK], BF16, "ExternalOutput")
    KT = P.dram("KT", [D, TOK], BF16, "ExternalOutput")
    Vo = P.dram("V", [TOK, D], BF16, "ExternalOutput")

    P.ring_init()
    P.consts()
    P.ffn_setup()
    gains = P.sb("gains_sb", [128, 3, 8], F32)
    wcm = P.sb("wcm_sb", [128, 3, 8], F32)
    wcf = P.sb("wcf_sb", [128, 3, 44], F32)
    P.op("sp", lambda e: e.dma_start(out=gains[:], in_=gains_d[:, :, :]), writes=["gains"], dma="c0")
    P.op("sp", lambda e: e.dma_start(out=wcm[:], in_=wcm_d[:, :, :]), writes=["wc"], dma="c1")
    P.op("sp", lambda e: e.dma_start(out=wcf[:], in_=wcf_d[:, :, :]), writes=["wc"], dma="c2")

    xts = [P.sb("xt%d" % i, [128, 8, 512], F32) for i in range(2)]
    hT = P.sb("hT", [128, 8, 512], BF16)
    yT = P.sb("yT", [128, 8, 512], BF16)
    cbuf = [P.sb("cbuf%d" % i, [128, 512], F32) for i in range(2)]
    cvb = [P.sb("cvb%d" % i, [128, 514], F32) for i in range(2)]
    tm = [P.sb("tm%d" % i, [128, 512], F32) for i in range(2)]
    hsm = P.sb("hsm", [128, 8, 2], F32)
    P.op("dve", lambda e: e.memset(hsm[:], 0.0), writes=[("hsm", i) for i in range(8)])

    inv_row = P.sb("inv_row", [1, 128], F32)
    half = HD // 2
    inv = (np.float32(10000.0) ** (-(np.arange(half, dtype=np.float32) / np.float32(half)))).astype(np.float32)
    for i in range(half):
        P.op("dve", (lambda e, i=i: e.memset(inv_row[0:1, i:128:32], float(inv[i]))), writes=["inv_row"])
    sgn = P.sb("sgn", [128, 1], F32)
    for q in range(4):
        P.op("dve", (lambda e, q=q: e.memset(sgn[32 * q:32 * q + 32, :], -1.0 if q % 2 == 0 else 1.0)), writes=["sgn"])
    pos_sb = P.sb("pos_sb", [1, TOK], F32)
    P.op("sp", lambda e: e.dma_start(out=pos_sb[:], in_=pos[:, :]), writes=["pos"], dma="c3")
    ang = P.sb("ang", [128, 512], F32)
    kf = P.sb("kf", [128, 512], F32)
    ki = P.sb("ki", [128, 512], mybir.dt.int32)
    cosT = P.sb("cosT", [128, 512], F32)
    sinT = P.sb("sinT", [128, 512], F32)
    rt1 = [P.sb("rt1_%d" % i, [128, 512], F32) for i in range(2)]
    rxs = [P.sb("rxs_%d" % i, [128, 512], F32) for i in range(2)]
    qkb = [P.sb("qkb%d" % i, [128, 512], BF16) for i in range(2)]
    vtok = [P.sb("vtok%d" % i, [128, 512], BF16) for i in range(2)]

    def mixer_blocks():
        bl = []
        for j in range(4):
            segs = []
            for si in range(3):
                src = w_in[:, si * D + 256 * j: si * D + 256 * j + 256].rearrange("(k p) n -> p k n", p=128)
                segs.append((256 * si, 8, 256, 768, src))
            bl.append(("in%d" % j, segs))
        return bl

    def qkv_blocks():
        bl = []
        for j in range(6):
            src = w_qkv[:, 512 * j:512 * j + 512].rearrange("(k p) n -> p k n", p=128)
            bl.append(("qkv%d" % j, [(0, 8, 512, 512, src)]))
        return bl

    for ti, (off, W) in enumerate(TILES):
        P.ring_plan(mixer_blocks() + Prog.sq_blocks(w_out, "wo") + Prog.ffn_blocks(w_up, w_down))
        if ti > 0:
            P.ring_plan(qkv_blocks())

    TWO_PI = 2.0 * np.pi
    C1 = float(np.float32(6.28125))
    C2 = float(np.float32(TWO_PI - 6.28125))

    rope_i = [0]
    tri_i = [0]
    import os
    dbg = int(os.environ.get("DBG_STOP", "99"))
    for ti, (off, W) in enumerate(TILES):
        xt = xts[ti % 2]
        xn = "xt%d" % (ti % 2)
        P.op("sp", (lambda e, xt=xt, off=off, W=W: e.dma_start(out=xt[:, :, 0:W], in_=xT[:, :, off:off + W])),
             writes=[(xn, c) for c in range(8)], dma="xin%d" % (ti % 2))
        if dbg == 0:
            P.op("sp", (lambda e, xt=xt, off=off, W=W: e.dma_start(out=x1T[:, :, off:off + W], in_=xt[:, :, 0:W])),
                 reads=[(xn, c) for c in range(8)], dma="xout%d" % (ti % 2))
            continue
        P.rmsnorm_stats(xt, xn, W, [0, 1, 2, 3, 4, 5])
        P.rmsnorm_apply(xt, xn, W, gains[:, 0, :], hT, "hT")
        if dbg == 1:
            P.op("sp", (lambda e, xt=xt, off=off, W=W: e.dma_start(out=x1T[:, :, off:off + W], in_=xt[:, :, 0:W])),
                 reads=[(xn, c) for c in range(8)] + [("hT", c) for c in range(8)], dma="xout%d" % (ti % 2))
            continue
        pend = None
        for j in range(4):
            slot, wn = P.ring_next("in%d" % j)
            for s in range(2):
                cj = 2 * j + s
                pi = tri_i[0] % 2
                tri_i[0] += 1
                bks = [P.bank("A", [0, 1, 2, 3, 4, 5]) for _ in range(3)]
                for si in (1, 2, 0):
                    bk = bks[si]
                    for kc in range(8):
                        lw = slot[:, kc * 768 + si * 256 + s * 128: kc * 768 + si * 256 + s * 128 + 128]
                        P.op("pe", (lambda e, bk=bk, lw=lw, kc=kc, W=W: e.matmul(P.ps[bk][:, 0:W], lhsT=lw, rhs=hT[:, kc, 0:W],
                                                                                 start=(kc == 0), stop=(kc == 7))),
                             reads=wn + [("hT", kc)], writes=[("ps", bk)])
                cb, cv, t = cbuf[pi], cvb[pi], tm[pi]
                P.op("act", (lambda e, cb=cb, bk=bks[1], W=W: e.activation(out=cb[:, 0:W], in_=P.ps[bk][:, 0:W], func=AF.Copy)),
                     reads=[("ps", bks[1])], writes=["cbuf%d" % pi])
                P.op("dve", (lambda e, cb=cb, cv=cv, bk=bks[2], W=W: e.tensor_tensor(out=cv[:, 2:2 + W], in0=cb[:, 0:W],
                                                                                      in1=P.ps[bk][:, 0:W], op=ALU.mult)),
                     reads=["cbuf%d" % pi, ("ps", bks[2])], writes=[("cvb%d" % pi, "m")])
                P.conv3(cv, "cvb%d" % pi, t, "tm%d" % pi, W, wcm, cj, hsm, "hsm")
                P.op("dve", (lambda e, t=t, cj=cj, bk=bks[0], W=W: e.tensor_tensor(out=yT[:, cj, 0:W], in0=t[:, 0:W],
                                                                                    in1=P.ps[bk][:, 0:W], op=ALU.mult)),
                     reads=["tm%d" % pi, ("ps", bks[0])], writes=[("yT", cj)])
            P.ring_done()
        if dbg == 2:
            P.op("sp", (lambda e, xt=xt, off=off, W=W: e.dma_start(out=x1T[:, :, off:off + W], in_=xt[:, :, 0:W])),
                 reads=[(xn, c) for c in range(8)] + [("yT", c) for c in range(8)], dma="xout%d" % (ti % 2))
            continue
        P.proj_add(xt, xn, yT, "yT", W, "wo", [6, 7])
        if dbg == 3:
            P.op("sp", (lambda e, xt=xt, off=off, W=W: e.dma_start(out=x1T[:, :, off:off + W], in_=xt[:, :, 0:W])),
                 reads=[(xn, c) for c in range(8)], dma="xout%d" % (ti % 2))
            continue
        P.rmsnorm_stats(xt, xn, W, [0, 1, 2, 3])
        P.rmsnorm_apply(xt, xn, W, gains[:, 1, :], hT, "hT")
        P.ffn(xt, xn, hT, W, wcf)
        P.op("sp", (lambda e, xt=xt, off=off, W=W: e.dma_start(out=x1T[:, :, off:off + W], in_=xt[:, :, 0:W])),
             reads=[(xn, c) for c in range(8)], dma="xout%d" % (ti % 2))
        if ti == 0 or dbg == 4:
            if ti > 0:
                P.cons += 6; P.released += 6
            continue
        m0 = off - HALO
        P.rmsnorm_stats(xt, xn, W, [0, 1, 2, 3])
        P.rmsnorm_apply(xt, xn, W, gains[:, 2, :], hT, "hT")
        bk = P.bank("A", [0, 1, 2, 3])
        P.op("pe", (lambda e, bk=bk, m0=m0: e.matmul(P.ps[bk][:, 0:512], lhsT=inv_row[:], rhs=pos_sb[0:1, m0:m0 + 512],
                                                      start=True, stop=True)),
             reads=["inv_row", "pos"], writes=[("ps", bk)])
        P.op("dve", (lambda e, bk=bk: e.tensor_copy(out=ang[:], in_=P.ps[bk][:, :])), reads=[("ps", bk)], writes=["ang"])
        for which, dst in (("sin", sinT), ("cos", cosT)):
            if which == "cos":
                P.op("dve", (lambda e: e.tensor_scalar(out=ang[:], in0=ang[:], scalar1=float(np.pi / 2), scalar2=None, op0=ALU.add)),
                     reads=["ang"], writes=["ang"])
            P.op("dve", (lambda e: e.tensor_scalar(out=ki[:], in0=ang[:], scalar1=float(1.0 / TWO_PI), scalar2=None, op0=ALU.mult)),
                 reads=["ang"], writes=["ki"])
            P.op("dve", (lambda e: e.tensor_copy(out=kf[:], in_=ki[:])), reads=["ki"], writes=["kf"])
            P.op("dve", (lambda e, dst=dst: e.scalar_tensor_tensor(out=dst[:], in0=kf[:], scalar=-C1, in1=ang[:], op0=ALU.mult, op1=ALU.add)),
                 reads=["kf", "ang"], writes=[which])
            P.op("dve", (lambda e, dst=dst: e.scalar_tensor_tensor(out=dst[:], in0=kf[:], scalar=-C2, in1=dst[:], op0=ALU.mult, op1=ALU.add)),
                 reads=["kf", which], writes=[which])
            P.op("dve", (lambda e, dst=dst: e.tensor_scalar(out=kf[:], in0=dst[:], scalar1=float(np.pi), scalar2=-TWO_PI, op0=ALU.is_gt, op1=ALU.mult)),
                 reads=[which], writes=["kf"])
            P.op("dve", (lambda e, dst=dst: e.tensor_tensor(out=dst[:], in0=dst[:], in1=kf[:], op=ALU.add)),
                 reads=[which, "kf"], writes=[which])
            P.op("dve", (lambda e, dst=dst: e.tensor_scalar(out=kf[:], in0=dst[:], scalar1=float(-np.pi), scalar2=TWO_PI, op0=ALU.is_lt, op1=ALU.mult)),
                 reads=[which], writes=["kf"])
            P.op("dve", (lambda e, dst=dst: e.tensor_tensor(out=dst[:], in0=dst[:], in1=kf[:], op=ALU.add)),
                 reads=[which, "kf"], writes=[which])
            if which == "sin":
                P.op("act", (lambda e: e.activation(out=sinT[:], in_=sinT[:], func=AF.Sin, scale=sgn[:, 0:1])),
                     reads=["sin", "sgn"], writes=["sin"])
            else:
                P.op("act", (lambda e: e.activation(out=cosT[:], in_=cosT[:], func=AF.Sin)), reads=["cos"], writes=["cos"])
        if dbg == 5:
            P.cons += 6; P.released += 6
            continue
        for j in range(4):
            slot, wn = P.ring_next("qkv%d" % j)
            scale = 0.125 if j < 2 else 1.0
            dstT = QT if j < 2 else KT
            for o in range(4):
                ch = (j % 2) * 4 + o
                ri = rope_i[0] % 2
                rope_i[0] += 1
                bk = P.bank("A", [0, 1, 2, 3])
                for kc in range(8):
                    lw = slot[:, kc * 512 + o * 128: kc * 512 + o * 128 + 128]
                    P.op("pe", (lambda e, bk=bk, lw=lw, kc=kc: e.matmul(P.ps[bk][:, :], lhsT=lw, rhs=hT[:, kc, :],
                                                                         start=(kc == 0), stop=(kc == 7))),
                         reads=wn + [("hT", kc)], writes=[("ps", bk)])
                t1, xs, ob = rt1[ri], rxs[ri], qkb[ri]
                dq = int(os.environ.get("DBG_QK", "9"))
                if dq >= 2:
                    for q in range(4):
                        src = 32 * (q ^ 1) if not os.environ.get("DBG_NOSHIFT") else 32 * q
                        P.op("act", (lambda e, xs=xs, q=q, src=src, bk=bk, scale=scale: e.activation(
                            out=xs[32 * q:32 * q + 32, :], in_=P.ps[bk][src:src + 32, :], func=AF.Copy, scale=scale)),
                             reads=[("ps", bk)], writes=[("rxs%d" % ri, q), ("psr", bk)])
                if dq >= 3:
                    P.op("dve", (lambda e, t1=t1, bk=bk, scale=scale: e.scalar_tensor_tensor(out=t1[:], in0=P.ps[bk][:, :], scalar=scale,
                                                                                              in1=cosT[:], op0=ALU.mult, op1=ALU.mult)),
                         reads=[("ps", bk), "cos", ("psr", bk)], writes=["rt1%d" % ri])
                if dq >= 4:
                    P.op("dve", (lambda e, xs=xs: e.tensor_tensor(out=xs[:], in0=xs[:], in1=sinT[:], op=ALU.mult)),
                         reads=[("rxs%d" % ri, q) for q in range(4)] + ["sin"], writes=[("rxs%d" % ri, q) for q in range(4)])
                if dq >= 5:
                    P.op("dve", (lambda e, t1=t1, xs=xs, ob=ob: e.tensor_tensor(out=ob[:], in0=t1[:], in1=xs[:], op=ALU.add)),
                         reads=["rt1%d" % ri] + [("rxs%d" % ri, q) for q in range(4)], writes=["qkb%d" % ri])
                else:
                    P.op("dve", (lambda e, ob=ob, bk=bk: e.tensor_copy(out=ob[:], in_=P.ps[bk][:, :])),
                         reads=[("ps", bk)], writes=["qkb%d" % ri])
                if not os.environ.get("DBG_NOSTORE"):
                    P.op("sp", (lambda e, ob=ob, dstT=dstT, ch=ch, m0=m0: e.dma_start(out=dstT[128 * ch:128 * ch + 128, m0:m0 + 512], in_=ob[:])),
                         reads=["qkb%d" % ri], dma="qk%d" % ri)
            P.ring_done()
        if dbg == 6:
            P.cons += 2; P.released += 2
            continue
        vi = 0
        slots_v = [P.ring_next("qkv4"), P.ring_next("qkv5")]
        for ts in range(4):
            for hf in range(2):
                slot, wn = slots_v[hf]
                bk = P.bank("A", [0, 1, 2, 3])
                for kc in range(8):
                    P.op("pe", (lambda e, bk=bk, slot=slot, kc=kc, ts=ts: e.matmul(P.ps[bk][:, :], lhsT=hT[:, kc, 128 * ts:128 * ts + 128],
                                                                                   rhs=slot[:, kc * 512:kc * 512 + 512],
                                                                                   start=(kc == 0), stop=(kc == 7))),
                         reads=wn + [("hT", kc)], writes=[("ps", bk)])
                vb = vtok[vi % 2]
                vn = "vtok%d" % (vi % 2)
                P.op("act", (lambda e, vb=vb, bk=bk: e.activation(out=vb[:], in_=P.ps[bk][:, :], func=AF.Copy)),
                     reads=[("ps", bk)], writes=[vn])
                P.op("sp", (lambda e, vb=vb, ts=ts, hf=hf, m0=m0: e.dma_start(
                    out=Vo[m0 + 128 * ts:m0 + 128 * ts + 128, 512 * hf:512 * hf + 512], in_=vb[:])),
                     reads=[vn], dma="v%d" % (vi % 2))
                vi += 1
        P.ring_done(2)
    assert dbg < 99 or P.cons == len(P.blocks)
    P.S.emit()
    return nc


NQT = SEQ // 512


def build_B(nheads=4, nqt=NQT):
    P = Prog()
    nc = P.nc
    QTh = P.dram("QTh", [4, 64, SEQ], BF16, "ExternalInput")
    KTh = P.dram("KTh", [4, 64, SEQ], BF16, "ExternalInput")
    Vh = P.dram("Vh", [4, SEQ, 64], BF16, "ExternalInput")
    OTh = P.dram("OTh", [4, 64, SEQ], BF16, "ExternalOutput")
    I32 = mybir.dt.int32

    KA = [P.sb("KA%d" % i, [128, SEQ], BF16) for i in range(2)]
    VA = [P.sb("VA%d" % i, [128, 128, 128], BF16) for i in range(2)]
    QA = [P.sb("QA%d" % i, [128, 512], BF16) for i in range(4)]
    kms = P.sb("kms", [64, 64], F32)
    kmT = [P.sb("kmT%d" % i, [64, 64], BF16) for i in range(2)]
    gsb = [P.sb("gsb%d" % i, [128, 64], F32) for i in range(4)]
    m8 = [P.sb("m8_%d" % i, [128, 8], F32) for i in range(4)]
    negp = [P.sb("negp%d" % i, [128, 128], BF16) for i in range(4)]
    ident = P.sb("ident", [128, 128], BF16)
    TRI = [P.sb("TRI%d" % i, [128, 512], BF16) for i in range(4)]
    PT = [P.sb("PT%d" % i, [128, 512], BF16) for i in range(4)]
    rden = [P.sb("rden%d" % i, [64, 512], F32) for i in range(2)]
    OTb = [P.sb("OTb%d" % i, [64, 512], BF16) for i in range(2)]

    for i in range(2):
        P.op("pool", (lambda e, i=i: e.iota(KA[i][64:128, :], [[1, 64], [0, 256]], base=0, channel_multiplier=-1,
                                            allow_small_or_imprecise_dtypes=True)), writes=[("KAE", i)])
        P.op("dve", (lambda e, i=i: e.tensor_single_scalar(out=KA[i][64:128, :], in_=KA[i][64:128, :], scalar=0.0, op=ALU.is_equal)),
             reads=[("KAE", i)], writes=[("KAE", i)])
        P.op("dve", (lambda e, i=i: e.memset(VA[i][:, :, 64:128], 1.0)), writes=[("VA1", i)])
    P.op("pool", (lambda e: e.iota(ident[:], [[1, 128]], base=0, channel_multiplier=-1, allow_small_or_imprecise_dtypes=True)),
         writes=["ident"])
    P.op("dve", (lambda e: e.tensor_single_scalar(out=ident[:], in_=ident[:], scalar=0.0, op=ALU.is_equal)),
         reads=["ident"], writes=["ident"])
    for kk in range(4):
        P.op("pool", (lambda e, kk=kk: e.iota(TRI[kk][:], [[1, 512]], base=-128 * kk, channel_multiplier=-1,
                                              allow_small_or_imprecise_dtypes=True)), writes=[("TRI", kk)])
        P.op("dve", (lambda e, kk=kk: e.tensor_scalar(out=TRI[kk][:], in0=TRI[kk][:], scalar1=0.0, scalar2=NEG,
                                                      op0=ALU.is_lt, op1=ALU.mult)), reads=[("TRI", kk)], writes=[("TRI", kk)])
    for i in range(4):
        P.op("dve", (lambda e, i=i: e.memset(negp[i][:], 0.0)), writes=[("negp", i)])

    S_BANKS = [0, 1, 2, 3]
    O_BANKS = [4, 5]
    G_BANK = 6
    T_BANK = 7

    def load_head(h):
        i = h % 2
        P.op("sp", (lambda e: e.dma_start(out=KA[i][0:64, :], in_=KTh[h, :, :])), writes=[("KAK", i)], dma="ka%d" % i)
        vsrc = Vh[h, :, :].rearrange("(kt p) d -> p kt d", p=128)
        for q in range(4):
            P.op("sp", (lambda e, q=q: e.dma_start(out=VA[i][:, vch * q:vch * q + vch, 0:64], in_=vsrc[:, vch * q:vch * q + vch, :])),
                 writes=[("VAV", i, q)], dma="va%d_%d" % (i, q))

    def head_prep(h):
        i = h % 2
        kview = KA[i][0:64, :].rearrange("p (b k) -> p b k", k=256)
        P.op("dve", (lambda e: e.tensor_reduce(out=kms[:], in_=kview, axis=mybir.AxisListType.X, op=ALU.add)),
             reads=[("KAK", i)], writes=["kms"])
        P.op("dve", (lambda e: e.tensor_scalar(out=kmT[i][:], in0=kms[:], scalar1=1.0 / 256.0, scalar2=None, op0=ALU.mult)),
             reads=["kms"], writes=[("kmT", i)])
        for s in range(4):
            P.op("dve", (lambda e, s=s: e.memset(gsb[s][:], -1e30)), writes=[("gsb", s)])

    items = [(h, t) for h in range(nheads) for t in range(nqt)]
    qa_of = {}

    def qload(n):
        if n >= len(items):
            return
        h, t = items[n]
        qi = n % 4
        qa = QA[qi]
        qa_of[n] = qi
        P.op("sp", (lambda e: e.dma_start(out=qa[0:64, :], in_=QTh[h, :, 512 * t:512 * t + 512])),
             writes=[("QAq", qi)], dma="qa%d" % qi)

    def gate1(n):
        h, t = items[n]
        i = h % 2
        qi = qa_of[n]
        qa = QA[qi]
        if t == 0:
            head_prep(h)
        for s in range(4):
            ob = 2 * t + s // 2
            P.op("pe", (lambda e, s=s: e.matmul(P.ps[G_BANK][:, 64 * s:64 * s + 64], lhsT=qa[0:64, 128 * s:128 * s + 128],
                                                 rhs=kmT[i][:, :], start=True, stop=True)),
                 reads=[("QAq", qi), ("kmT", i)], writes=[("ps", G_BANK)])
        for s in range(4):
            ob = 2 * t + s // 2
            if ob > 0:
                P.op("dve", (lambda e, s=s, ob=ob: e.tensor_copy(out=gsb[s][:, 0:ob], in_=P.ps[G_BANK][:, 64 * s:64 * s + ob])),
                     reads=[("ps", G_BANK)], writes=[("gsb", s)])
            P.op("dve", (lambda e, s=s, ob=ob: e.memset(gsb[s][:, ob:ob + 1], 1e30)), writes=[("gsb", s)])
            P.op("dve", (lambda e, s=s: e.max(out=m8[s][:], in_=gsb[s][:, :])), reads=[("gsb", s)], writes=[("m8", s)])
            P.op("dve", (lambda e, s=s: e.tensor_scalar(out=negp[s][:, 64:128], in0=gsb[s][:, :], scalar1=m8[s][:, 3:4], scalar2=NEG,
                                                        op0=ALU.is_lt, op1=ALU.mult)),
                 reads=[("gsb", s), ("m8", s)], writes=[("negp", s)])

    def gate2(n):
        qi = qa_of[n]
        qa = QA[qi]
        for s in range(4):
            P.op("pe", (lambda e, s=s: e.matmul(P.ps[T_BANK][:, 128 * s:128 * s + 128], lhsT=negp[s][:, :], rhs=ident[:, :],
                                                 start=True, stop=True)),
                 reads=[("negp", s), "ident"], writes=[("ps", T_BANK)])
        P.op("dve", (lambda e: e.tensor_copy(out=qa[64:128, :], in_=P.ps[T_BANK][64:128, :])),
             reads=[("ps", T_BANK)], writes=[("QAn", qi)])

    pt_i = [0]

    def main(n):
        h, t = items[n]
        i = h % 2
        qi = qa_of[n]
        qa = QA[qi]
        nk = 4 * t + 4
        obk = O_BANKS[n % 2]
        live = {}
        for step in range(nk + 2):
            if step < nk:
                kt = step
                sb_ = P.bank("S", S_BANKS)
                diag = kt >= 4 * t
                P.op("pe", (lambda e, kt=kt, sb_=sb_, diag=diag: e.matmul(P.ps[sb_][:, :], lhsT=KA[i][:, 128 * kt:128 * kt + 128], rhs=qa[:, :],
                                                                          start=True, stop=(not diag))),
                     reads=[("KAK", i), ("KAE", i), ("QAq", qi), ("QAn", qi)], writes=[("ps", sb_)])
                if diag:
                    P.op("pe", (lambda e, kt=kt, sb_=sb_: e.matmul(P.ps[sb_][:, :], lhsT=ident[:, :], rhs=TRI[kt - 4 * t][:, :],
                                                                   start=False, stop=True)),
                         reads=["ident", ("TRI", kt - 4 * t)], writes=[("ps", sb_)])
                pi = pt_i[0] % 4
                pt_i[0] += 1
                P.op("act", (lambda e, pi=pi, sb_=sb_: e.activation(out=PT[pi][:, :], in_=P.ps[sb_][:, :], func=AF.Exp)),
                     reads=[("ps", sb_)], writes=[("PT", pi)])
                live[kt] = pi
            if step >= 2:
                kt = step - 2
                pi = live.pop(kt)
                P.op("pe", (lambda e, kt=kt, pi=pi: e.matmul(P.ps[obk][:, :], lhsT=VA[i][:, kt, :], rhs=PT[pi][:, :],
                                                              start=(kt == 0), stop=(kt == nk - 1))),
                     reads=[("VAV", i, kt // vch), ("VA1", i), ("PT", pi)], writes=[("ps", obk)])
            if step == 0:
                qload(n + 2)
            if step == 1 and n + 1 < len(items):
                gate1(n + 1)
            if step == max(nk - 1, 2) and n + 1 < len(items):
                gate2(n + 1)
        r = n % 2
        P.op("dve", (lambda e: e.reciprocal(out=rden[r][:, :], in_=P.ps[obk][64:128, :])), reads=[("ps", obk)], writes=[("rden", r)])
        P.op("dve", (lambda e: e.tensor_tensor(out=OTb[r][:, :], in0=P.ps[obk][0:64, :], in1=rden[r][:, :], op=ALU.mult)),
             reads=[("ps", obk), ("rden", r)], writes=[("OTb", r)])
        P.op("sp", (lambda e: e.dma_start(out=OTh[h, :, 512 * t:512 * t + 512], in_=OTb[r][:, :])), reads=[("OTb", r)], dma="ot%d" % r)

    load_head(0)
    qload(0)
    qload(1)
    gate1(0)
    gate2(0)
    for n in range(len(items)):
        h, t = items[n]
        if t == 0 and h + 1 < nheads:
            load_head(h + 1)
        main(n)
    P.S.emit()
    return nc


def build_C():
    P = Prog()
    nc = P.nc
    x1T = P.dram("x1T", [128, 8, NT], F32, "ExternalInput")
    OTt = P.dram("OTt", [128, 8, NT], BF16, "ExternalInput")
    gains_d = P.dram("gains", [128, 2, 8], F32, "ExternalInput")
    wcf_d = P.dram("wcf", [128, 3, 44], F32, "ExternalInput")
    w_o = P.dram("w_o", [D, D], F32, "ExternalInput")
    w_up = P.dram("w_up", [D, 2 * DFF], F32, "ExternalInput")
    w_down = P.dram("w_down", [DFF, D], F32, "ExternalInput")
    outT = P.dram("outT", [128, 8, TOK], F32, "ExternalOutput")

    P.ring_init()
    P.consts()
    P.ffn_setup()
    gains = P.sb("gains_sb", [128, 2, 8], F32)
    wcf = P.sb("wcf_sb", [128, 3, 44], F32)
    P.op("sp", lambda e: e.dma_start(out=gains[:], in_=gains_d[:, :, :]), writes=["gains"], dma="c0")
    P.op("sp", lambda e: e.dma_start(out=wcf[:], in_=wcf_d[:, :, :]), writes=["wc"], dma="c2")
    xts = [P.sb("xt%d" % i, [128, 8, 512], F32) for i in range(2)]
    ots = [P.sb("ot%d" % i, [128, 8, 512], BF16) for i in range(2)]
    yo = [P.sb("yo%d" % i, [128, 8, 512], F32) for i in range(2)]
    hT = P.sb("hT", [128, 8, 512], BF16)
    for ti, (off, W) in enumerate(TILES):
        P.ring_plan(Prog.sq_blocks(w_o, "wo") + Prog.ffn_blocks(w_up, w_down))
    for ti, (off, W) in enumerate(TILES):
        xt = xts[ti % 2]
        xn = "xt%d" % (ti % 2)
        ot = ots[ti % 2]
        on = "ot%d" % (ti % 2)
        P.op("sp", (lambda e, xt=xt, off=off, W=W: e.dma_start(out=xt[:, :, 0:W], in_=x1T[:, :, off:off + W])),
             writes=[(xn, c) for c in range(8)], dma="xin%d" % (ti % 2))
        P.op("sp", (lambda e, ot=ot, off=off, W=W: e.dma_start(out=ot[:, :, 0:W], in_=OTt[:, :, off:off + W])),
             writes=[(on, c) for c in range(8)], dma="oin%d" % (ti % 2))
        P.proj_add(xt, xn, ot, on, W, "wo", [6, 7])
        P.rmsnorm_stats(xt, xn, W, [0, 1, 2, 3])
        P.rmsnorm_apply(xt, xn, W, gains[:, 0, :], hT, "hT")
        P.ffn(xt, xn, hT, W, wcf)
        if ti == 0:
            continue
        m0 = off - HALO
        y = yo[ti % 2]
        yn = "yo%d" % (ti % 2)
        P.rmsnorm_stats(xt, xn, W, [0, 1, 2, 3])
        P.rmsnorm_apply(xt, xn, W, gains[:, 1, :], y, yn)
        P.op("sp", (lambda e, y=y, m0=m0: e.dma_start(out=outT[:, :, m0:m0 + 512], in_=y[:, :, :])),
             reads=[(yn, c) for c in range(8)], dma="yout%d" % (ti % 2))
    assert P.cons == len(P.blocks)
    P.S.emit()
    return nc


NTL = SEQ // 512
OWN0 = 24
NSL = 9


OWN_TILES = [12, 13, 14, 15, 28, 29, 30, 31]


def build_fused(n_tiles=NTL, n_heads=NH, own_tiles=None):
    own_tiles = sorted(OWN_TILES if own_tiles is None else own_tiles)
    halo_tiles = sorted(T - 1 for T in own_tiles if T - 1 not in own_tiles)
    assert all(T >= 0 for T in halo_tiles)
    slot_of = {T: i for i, T in enumerate(sorted(set(own_tiles) | set(halo_tiles)))}
    out_of = {T: i for i, T in enumerate(own_tiles)}
    P = Prog()
    P.presq = True
    P.use_wscr = True
    nc = P.nc
    nsl = len(slot_of)
    L = n_tiles * 512
    xT = P.dram("xT", [128, 8, L], F32, "ExternalInput")
    pos = P.dram("pos", [1, L], F32, "ExternalInput")
    gbias_d = P.dram("gbias", [128, 64], F32, "ExternalInput")
    P.bar_src = pos[0:1, 0:64]
    gains_d = P.dram("gains", [128, 5, 8], F32, "ExternalInput")
    wcm_d = P.dram("wcm", [128, 3, 8], F32, "ExternalInput")
    wcf0_d = P.dram("wcf0", [128, 3, 44], F32, "ExternalInput")
    wcf1_d = P.dram("wcf1", [128, 3, 44], F32, "ExternalInput")
    w_in = P.dram("w_in", [D, 3 * D], F32, "ExternalInput")
    w_out = P.dram("w_out", [D, D], F32, "ExternalInput")
    w_up0 = P.dram("w_up0", [D, 2 * DFF], F32, "ExternalInput")
    w_down0 = P.dram("w_down0", [DFF, D], F32, "ExternalInput")
    w_qkv = P.dram("w_qkv", [D, 3 * D], F32, "ExternalInput")
    w_o = P.dram("w_o", [D, D], F32, "ExternalInput")
    w_up1 = P.dram("w_up1", [D, 2 * DFF], F32, "ExternalInput")
    w_down1 = P.dram("w_down1", [DFF, D], F32, "ExternalInput")
    outT = P.dram("outT", [128, 8, TOK], F32, "ExternalOutput")
    x1s = P.dram("x1s", [128, 8, nsl * 512], F32, "Internal")
    KTs = P.dram("KTs", [D, L], BF16, "Internal")
    Vs = P.dram("Vs", [L, D], BF16, "Internal")
    QTs = P.dram("QTs", [D, nsl * 512], BF16, "Internal")
    OTs = P.dram("OTs", [D, nsl * 512], BF16, "Internal")
    half = HD // 2
    inv = (np.float32(10000.0) ** (-(np.arange(half, dtype=np.float32) / np.float32(half)))).astype(np.float32)
    TWO_PI = 2.0 * np.pi
    C1 = float(np.float32(6.28125))
    C2 = float(np.float32(TWO_PI - 6.28125))

    P.phase_begin()
    P.ring_init()
    P.consts()
    P.ffn_setup()
    gains = P.sb("gains_sb", [128, 5, 8], F32)
    wcm = P.sb("wcm_sb", [128, 3, 8], F32)
    wcf = P.sb("wcf_sb", [128, 3, 44], F32)
    P.op("sp", lambda e: e.dma_start(out=gains[:], in_=gains_d[:, :, :]), writes=["gains"], dma="c0")
    P.op("sp", lambda e: e.dma_start(out=wcm[:], in_=wcm_d[:, :, :]), writes=["wc"], dma="c1")
    P.op("sp", lambda e: e.dma_start(out=wcf[:], in_=wcf0_d[:, :, :]), writes=["wc"], dma="c2")
    xts = [P.sb("xt%d" % i, [128, 8, 512], F32) for i in range(2)]
    hT = P.sb("hT", [128, 8, 512], BF16)
    hTm = P.sb("hTm", [128, 8, 512], BF16)
    sqn = P.sqn
    yT = P.sb("yT", [128, 8, 512], BF16)
    cbuf = [P.sb("cbuf%d" % i, [128, 512], F32) for i in range(2)]
    cvb = [P.sb("cvb%d" % i, [128, 514], F32) for i in range(2)]
    tm = [P.sb("tm%d" % i, [128, 512], F32) for i in range(2)]
    hsm = P.sb("hsm", [128, 8, 2], F32)
    P.op("dve", lambda e: e.memset(hsm[:], 0.0), writes=[("hsm", i) for i in range(8)])
    inv_row = P.sb("inv_row", [1, 128], F32)
    for i in range(half):
        P.op("dve", (lambda e, i=i: e.memset(inv_row[0:1, i:128:32], float(inv[i]))), writes=["inv_row"])
    sgn = P.sb("sgn", [128, 1], F32)
    for q in range(4):
        P.op("dve", (lambda e, q=q: e.memset(sgn[32 * q:32 * q + 32, :], -1.0 if q % 2 == 0 else 1.0)), writes=["sgn"])
    pos_sbs = [P.sb("pos_sb%d" % i, [1, 512], F32) for i in range(2)]
    ang = P.sb("ang", [128, 512], F32)
    kf = P.sb("kf", [128, 512], F32)
    ki = P.sb("ki", [128, 512], mybir.dt.int32)
    cosT = P.sb("cosT", [128, 512], F32)
    sinT = P.sb("sinT", [128, 512], F32)
    rt1 = [P.sb("rt1_%d" % i, [128, 512], F32) for i in range(2)]
    rxs = [P.sb("rxs_%d" % i, [128, 512], F32) for i in range(2)]
    rxc = [P.sb("rxc_%d" % i, [128, 512], F32) for i in range(2)]
    NOB = 3
    qkb = [P.sb("qkb%d" % i, [128, 512], BF16) for i in range(NOB)]
    vtok = [P.sb("vtok%d" % i, [128, 512], BF16) for i in range(NOB)]

    def mixer_blocks():
        bl = []
        for j in range(4):
            segs = []
            for si in range(3):
                src = w_in[:, si * D + 256 * j: si * D + 256 * j + 256].rearrange("(k p) n -> p k n", p=128)
                segs.append((256 * si, 8, 256, 768, src))
            bl.append(("in%d" % j, segs))
        return bl

    def qkv_blocks(with_q):
        bl = []
        for j in (range(6) if with_q else range(2, 6)):
            src = w_qkv[:, 512 * j:512 * j + 512].rearrange("(k p) n -> p k n", p=128)
            bl.append(("qkv%d" % j, [(0, 8, 512, 512, src)]))
        return bl

    for ti in range(n_tiles):
        P.ring_plan(mixer_blocks() + Prog.sq_blocks(w_out, "wo") + Prog.ffn_blocks(w_up0, w_down0) + qkv_blocks(ti in slot_of))
    rope_i = [0]
    tri_i = [0]
    W = 512
    for ti in range(n_tiles):
        off = ti * 512
        if ti > 0 and ti % 11 == 0:
            P.mid_barrier()
        xt = xts[ti % 2]
        xn = "xt%d" % (ti % 2)

        def xload(tj):
            if tj >= n_tiles:
                return
            P.op("sp", (lambda e: e.dma_start(out=xts[tj % 2][:, :, :], in_=xT[:, :, tj * 512:tj * 512 + 512])),
                 writes=[("xt%d" % (tj % 2), c) for c in range(8)], dma="xin%d" % (tj % 2))
        if ti == 0:
            xload(0)
        xload(ti + 1)

        def norm1_sq(tj):
            if tj >= n_tiles:
                return
            for c in range(8):
                P.op("act", (lambda e, c=c: e.activation(out=sqn[c][:, :], in_=xts[tj % 2][:, c, :], func=AF.Square)),
                     reads=[("xt%d" % (tj % 2), c)], writes=[("sqn", c)])

        def norm1_rest(tj):
            if tj >= n_tiles:
                return
            bk_ = P.bank("A", [0, 1, 2, 3])
            for c in range(8):
                P.op("pe", (lambda e, c=c, bk_=bk_: e.matmul(P.ps[bk_][:, :], lhsT=P.ones[:], rhs=sqn[c][:, :], start=(c == 0), stop=(c == 7))),
                     reads=[("sqn", c), "ones"], writes=[("ps", bk_)])
            P.op("dve", (lambda e, bk_=bk_: e.tensor_scalar(out=P.rsb[:, :], in0=P.ps[bk_][:, :], scalar1=1.0 / D, scalar2=EPS, op0=ALU.mult, op1=ALU.add)),
                 reads=[("ps", bk_)], writes=["rsb"])
            P.op("act", (lambda e: e.activation(out=P.rsb[:, :], in_=P.rsb[:, :], func=AF.Sqrt)), reads=["rsb"], writes=["rsb"])
            P.op("dve", (lambda e: e.reciprocal(out=P.rstd[:, :], in_=P.rsb[:, :])), reads=["rsb"], writes=["rstd"])
            P.rmsnorm_apply(xts[tj % 2], "xt%d" % (tj % 2), W, gains[:, 0, :], hTm, "hTm")
        if ti == 0:
            norm1_sq(0)
            norm1_rest(0)
        for j in range(4):
            slot, wn = P.ring_next("in%d" % j)
            for s_ in range(2):
                cj = 2 * j + s_
                pi = tri_i[0] % 2
                tri_i[0] += 1
                bks = [P.bank("A", [0, 1, 2, 3, 4, 5]) for _ in range(3)]
                for si in (1, 2, 0):
                    bk = bks[si]
                    for kc in range(8):
                        lw = slot[:, kc * 768 + si * 256 + s_ * 128: kc * 768 + si * 256 + s_ * 128 + 128]
                        P.op("pe", (lambda e, bk=bk, lw=lw, kc=kc: e.matmul(P.ps[bk][:, :], lhsT=lw, rhs=hTm[:, kc, :],
                                                                            start=(kc == 0), stop=(kc == 7))),
                             reads=wn + [("hTm", kc)], writes=[("ps", bk)])
                cb, cv, t = cbuf[pi], cvb[pi], tm[pi]
                P.op("act", (lambda e, cb=cb, bk=bks[1]: e.activation(out=cb[:, :], in_=P.ps[bk][:, :], func=AF.Copy)),
                     reads=[("ps", bks[1])], writes=["cbuf%d" % pi])
                P.op("dve", (lambda e, cb=cb, cv=cv, bk=bks[2]: e.tensor_tensor(out=cv[:, 2:514], in0=cb[:, :], in1=P.ps[bk][:, :], op=ALU.mult)),
                     reads=["cbuf%d" % pi, ("ps", bks[2])], writes=[("cvb%d" % pi, "m")])
                P.conv3(cv, "cvb%d" % pi, t, "tm%d" % pi, W, wcm, cj, hsm, "hsm")
                P.op("dve", (lambda e, t=t, cj=cj, bk=bks[0]: e.tensor_tensor(out=yT[:, cj, :], in0=t[:, :], in1=P.ps[bk][:, :], op=ALU.mult)),
                     reads=["tm%d" % pi, ("ps", bks[0])], writes=[("yT", cj)])
            P.ring_done()
        P.proj_add(xt, xn, yT, "yT", W, "wo", [6, 7])
        steps = []
        pos_sb = pos_sbs[ti % 2]

        def S_(eng, fn, **kw):
            steps.append(lambda eng=eng, fn=fn, kw=kw: P.op(eng, fn, **kw))
        S_("sp", (lambda e, pos_sb=pos_sb, off=off: e.dma_start(out=pos_sb[:], in_=pos[:, off:off + 512])), writes=[("pos", ti % 2)], dma="pos%d" % (ti % 2))
        S_("pe", (lambda e, pos_sb=pos_sb: e.matmul(P.ps[7][:, :], lhsT=inv_row[:], rhs=pos_sb[0:1, :], start=True, stop=True)),
           reads=["inv_row", ("pos", ti % 2)], writes=[("ps", 7)])
        S_("dve", (lambda e: e.tensor_copy(out=ang[:], in_=P.ps[7][:, :])), reads=[("ps", 7)], writes=["ang"])
        for which, dst in (("sin", sinT), ("cos", cosT)):
            if which == "cos":
                S_("dve", (lambda e: e.tensor_scalar(out=ang[:], in0=ang[:], scalar1=float(np.pi / 2), scalar2=None, op0=ALU.add)),
                   reads=["ang"], writes=["ang"])
            S_("dve", (lambda e: e.tensor_scalar(out=ki[:], in0=ang[:], scalar1=float(1.0 / TWO_PI), scalar2=None, op0=ALU.mult)),
               reads=["ang"], writes=["ki"])
            S_("dve", (lambda e: e.tensor_copy(out=kf[:], in_=ki[:])), reads=["ki"], writes=["kf"])
            S_("dve", (lambda e, dst=dst: e.scalar_tensor_tensor(out=dst[:], in0=kf[:], scalar=-C1, in1=ang[:], op0=ALU.mult, op1=ALU.add)),
               reads=["kf", "ang"], writes=[which])
            S_("dve", (lambda e, dst=dst: e.scalar_tensor_tensor(out=dst[:], in0=kf[:], scalar=-C2, in1=dst[:], op0=ALU.mult, op1=ALU.add)),
               reads=["kf", which], writes=[which])
            S_("dve", (lambda e, dst=dst: e.tensor_scalar(out=kf[:], in0=dst[:], scalar1=float(np.pi), scalar2=-TWO_PI, op0=ALU.is_gt, op1=ALU.mult)),
               reads=[which], writes=["kf"])
            S_("dve", (lambda e, dst=dst: e.tensor_tensor(out=dst[:], in0=dst[:], in1=kf[:], op=ALU.add)), reads=[which, "kf"], writes=[which])
            S_("dve", (lambda e, dst=dst: e.tensor_scalar(out=kf[:], in0=dst[:], scalar1=float(-np.pi), scalar2=TWO_PI, op0=ALU.is_lt, op1=ALU.mult)),
               reads=[which], writes=["kf"])
            S_("dve", (lambda e, dst=dst: e.tensor_tensor(out=dst[:], in0=dst[:], in1=kf[:], op=ALU.add)), reads=[which, "kf"], writes=[which])
            if which == "sin":
                S_("act", (lambda e: e.activation(out=sinT[:], in_=sinT[:], func=AF.Sin, scale=sgn[:, 0:1])), reads=["sin", "sgn"], writes=["sin"])
            else:
                S_("act", (lambda e: e.activation(out=cosT[:], in_=cosT[:], func=AF.Sin)), reads=["cos"], writes=["cos"])
        P.rmsnorm_stats_pre(W, [0, 1, 2, 3])
        P.rmsnorm_apply(xt, xn, W, gains[:, 1, :], hT, "hT")
        P.ffn(xt, xn, hT, W, wcf, hook=steps)
        with_q = ti in slot_of
        if with_q:
            so = slot_of[ti] * 512
            P.op("sp", (lambda e, xt=xt, so=so: e.dma_start(out=x1s[:, :, so:so + 512], in_=xt[:, :, :])),
                 reads=[(xn, c) for c in range(8)], dma="xout%d" % (ti % 2))
        P.rmsnorm_stats_pre(W, [0, 1, 2, 3])
        P.rmsnorm_apply(xt, xn, W, gains[:, 2, :], hT, "hT")
        for j in (range(4) if with_q else range(2, 4)):
            slot, wn = P.ring_next("qkv%d" % j)
            scale = 0.125 if j < 2 else 1.0
            for o in range(4):
                ch = (j % 2) * 4 + o
                ri = rope_i[0] % 2
                oi = rope_i[0] % NOB
                rope_i[0] += 1
                bk = P.bank("A", [0, 1, 2, 3])
                for kc in range(8):
                    lw = slot[:, kc * 512 + o * 128: kc * 512 + o * 128 + 128]
                    P.op("pe", (lambda e, bk=bk, lw=lw, kc=kc: e.matmul(P.ps[bk][:, :], lhsT=lw, rhs=hT[:, kc, :], start=(kc == 0), stop=(kc == 7))),
                         reads=wn + [("hT", kc)], writes=[("ps", bk)])
                t1, xs, xc, ob = rt1[ri], rxs[ri], rxc[ri], qkb[oi]
                P.op("act", (lambda e, xc=xc, bk=bk, scale=scale: e.activation(out=xc[:, :], in_=P.ps[bk][:, :], func=AF.Copy, scale=scale)),
                     reads=[("ps", bk)], writes=["rxc%d" % ri])
                for q in range(4):
                    src = 32 * (q ^ 1)
                    P.op("act", (lambda e, xs=xs, q=q, src=src, bk=bk, scale=scale: e.activation(
                        out=xs[32 * q:32 * q + 32, :], in_=P.ps[bk][src:src + 32, :], func=AF.Copy, scale=scale)),
                         reads=[("ps", bk)], writes=[("rxs%d" % ri, q)])
                P.op("dve", (lambda e, t1=t1, xc=xc: e.tensor_tensor(out=t1[:], in0=xc[:], in1=cosT[:], op=ALU.mult)),
                     reads=["rxc%d" % ri, "cos"], writes=["rt1%d" % ri])
                P.op("dve", (lambda e, xs=xs: e.tensor_tensor(out=xs[:], in0=xs[:], in1=sinT[:], op=ALU.mult)),
                     reads=[("rxs%d" % ri, q) for q in range(4)] + ["sin"], writes=[("rxs%d" % ri, q) for q in range(4)])
                P.op("dve", (lambda e, t1=t1, xs=xs, ob=ob: e.tensor_tensor(out=ob[:], in0=t1[:], in1=xs[:], op=ALU.add)),
                     reads=["rt1%d" % ri] + [("rxs%d" % ri, q) for q in range(4)], writes=["qkb%d" % oi])
                if j < 2:
                    so = slot_of[ti] * 512
                    P.op("sp", (lambda e, ob=ob, ch=ch, so=so: e.dma_start(out=QTs[128 * ch:128 * ch + 128, so:so + 512], in_=ob[:])),
                         reads=["qkb%d" % oi], dma="qk%d" % oi)
                else:
                    P.op("sp", (lambda e, ob=ob, ch=ch, off=off: e.dma_start(out=KTs[128 * ch:128 * ch + 128, off:off + 512], in_=ob[:])),
                         reads=["qkb%d" % oi], dma="qk%d" % oi)
            P.ring_done()
        vi = 0
        slots_v = [P.ring_next("qkv4"), P.ring_next("qkv5")]
        for ts in range(4):
            if ts == 1:
                norm1_sq(ti + 1)
            if ts == 3:
                norm1_rest(ti + 1)
            for hf in range(2):
                slot, wn = slots_v[hf]
                bk = P.bank("A", [0, 1, 2, 3])
                for kc in range(8):
                    P.op("pe", (lambda e, bk=bk, slot=slot, kc=kc, ts=ts: e.matmul(P.ps[bk][:, :], lhsT=hT[:, kc, 128 * ts:128 * ts + 128],
                                                                                   rhs=slot[:, kc * 512:kc * 512 + 512], start=(kc == 0), stop=(kc == 7))),
                         reads=wn + [("hT", kc)], writes=[("ps", bk)])
                vb = vtok[vi % NOB]
                vn = "vtok%d" % (vi % NOB)
                P.op("act", (lambda e, vb=vb, bk=bk: e.activation(out=vb[:], in_=P.ps[bk][:, :], func=AF.Copy)), reads=[("ps", bk)], writes=[vn])
                P.op("sp", (lambda e, vb=vb, ts=ts, hf=hf, off=off: e.dma_start(
                    out=Vs[off + 128 * ts:off + 128 * ts + 128, 512 * hf:512 * hf + 512], in_=vb[:])), reads=[vn], dma="v%d" % (vi % NOB))
                vi += 1
        P.ring_done(2)
    assert P.cons == len(P.blocks)
    P.phase_end()

    P.phase_begin()
    KA = [P.sb("KA%d" % i, [128, L], BF16) for i in range(2)]
    VA = [P.sb("VA%d" % i, [128, L // 128, 128], BF16) for i in range(2)]
    QA = [P.sb("QA%d" % i, [128, 512], BF16) for i in range(4)]
    kms = P.sb("kms", [64, 64], F32)
    kmT = [P.sb("kmT%d" % i, [64, 64], BF16) for i in range(2)]
    gsb = [P.sb("gsb%d" % i, [128, 64], F32) for i in range(4)]
    m8 = [P.sb("m8_%d" % i, [128, 8], F32) for i in range(4)]
    negp = [P.sb("negp%d" % i, [128, 128], BF16) for i in range(4)]
    ident = P.sb("ident", [128, 128], BF16)
    TRI = [P.sb("TRI%d" % i, [128, 512], BF16) for i in range(4)]
    PT = [P.sb("PT%d" % i, [128, 512], BF16) for i in range(4)]
    rden = [P.sb("rden%d" % i, [64, 512], F32) for i in range(2)]
    OTb = [P.sb("OTb%d" % i, [64, 512], BF16) for i in range(2)]
    gbias = P.sb("gbias_sb", [128, 64], F32)
    P.op("sp", lambda e: e.dma_start(out=gbias[:], in_=gbias_d[:, :]), writes=["gbias"], dma="c0")
    nblk = L // 256
    for i in range(2):
        P.op("pool", (lambda e, i=i: e.iota(KA[i][64:128, :], [[1, nblk], [0, 256]], base=0, channel_multiplier=-1,
                                            allow_small_or_imprecise_dtypes=True)), writes=[("KAE", i)])
        P.op("dve", (lambda e, i=i: e.tensor_single_scalar(out=KA[i][64:128, :], in_=KA[i][64:128, :], scalar=0.0, op=ALU.is_equal)),
             reads=[("KAE", i)], writes=[("KAE", i)])
        P.op("dve", (lambda e, i=i: e.memset(VA[i][:, :, 64:128], 1.0)), writes=[("VA1", i)])
    P.op("pool", (lambda e: e.iota(ident[:], [[1, 128]], base=0, channel_multiplier=-1, allow_small_or_imprecise_dtypes=True)), writes=["ident"])
    P.op("dve", (lambda e: e.tensor_single_scalar(out=ident[:], in_=ident[:], scalar=0.0, op=ALU.is_equal)), reads=["ident"], writes=["ident"])
    for kk in range(4):
        P.op("pool", (lambda e, kk=kk: e.iota(TRI[kk][:], [[1, 512]], base=-128 * kk, channel_multiplier=-1,
                                              allow_small_or_imprecise_dtypes=True)), writes=[("TRI", kk)])
        P.op("dve", (lambda e, kk=kk: e.tensor_scalar(out=TRI[kk][:], in0=TRI[kk][:], scalar1=0.0, scalar2=NEG, op0=ALU.is_lt, op1=ALU.mult)),
             reads=[("TRI", kk)], writes=[("TRI", kk)])
    for i in range(4):
        P.op("dve", (lambda e, i=i: e.memset(negp[i][:], 0.0)), writes=[("negp", i)])
    S_BANKS = [0, 1, 2, 3]
    O_BANKS = [4, 5]
    G_BANK = 6
    T_BANK = 7
    nkt_all = L // 128
    vch = min(32, nkt_all)
    nvq = nkt_all // vch

    def load_head(h):
        i = h % 2
        P.op("sp", (lambda e: e.dma_start(out=KA[i][0:64, :], in_=KTs[64 * h:64 * h + 64, :])), writes=[("KAK", i)], dma="ka%d" % i)
        vsrc = Vs[:, 64 * h:64 * h + 64].rearrange("(kt p) d -> p kt d", p=128)
        for q in range(nvq):
            P.op("sp", (lambda e, q=q: e.dma_start(out=VA[i][:, vch * q:vch * q + vch, 0:64], in_=vsrc[:, vch * q:vch * q + vch, :])),
                 writes=[("VAV", i, q)], dma="va%d_%d" % (i, q))

    def head_prep(h):
        i = h % 2
        kview = KA[i][0:64, :].rearrange("p (b k) -> p b k", k=256)
        P.op("dve", (lambda e: e.tensor_reduce(out=kms[:, 0:nblk], in_=kview, axis=mybir.AxisListType.X, op=ALU.add)),
             reads=[("KAK", i)], writes=["kms"])
        P.op("dve", (lambda e: e.tensor_scalar(out=kmT[i][:, 0:nblk], in0=kms[:, 0:nblk], scalar1=1.0 / 256.0, scalar2=None, op0=ALU.mult)),
             reads=["kms"], writes=[("kmT", i)])
        for s_ in range(4):
            P.op("dve", (lambda e, s_=s_: e.memset(gsb[s_][:], -1e30)), writes=[("gsb", s_)])

    items = []
    for h in range(n_heads):
        for T in sorted(slot_of):
            if T in out_of:
                items.append((h, T, 0, 512, slot_of[T] * 512, [(128 * s_, 128, 2 * T + s_ // 2) for s_ in range(4)]))
            else:
                items.append((h, T, 512 - HALO, HALO, slot_of[T] * 512, [(0, HALO, 2 * T + 1)]))
    qa_of = {}

    def qload(n):
        if n >= len(items):
            return
        h, T, c0, Wq, sc, subs = items[n]
        qi = n % 4
        qa_of[n] = qi
        P.op("sp", (lambda e: e.dma_start(out=QA[qi][0:64, 0:Wq], in_=QTs[64 * h:64 * h + 64, sc + c0:sc + c0 + Wq])),
             writes=[("QAq", qi)], dma="qa%d" % qi)

    def gate1(n):
        h, T, c0, Wq, sc, subs = items[n]
        i = h % 2
        qi = qa_of[n]
        qa = QA[qi]
        if n == 0 or items[n - 1][0] != h:
            head_prep(h)
        for s_, (o_, nq, ob) in enumerate(subs):
            P.op("pe", (lambda e, s_=s_, o_=o_, nq=nq: e.matmul(P.ps[G_BANK][0:nq, 64 * s_:64 * s_ + nblk], lhsT=qa[0:64, o_:o_ + nq],
                                                                 rhs=kmT[i][:, 0:nblk], start=True, stop=True)),
                 reads=[("QAq", qi), ("kmT", i)], writes=[("ps", G_BANK)])
        for s_, (o_, nq, ob) in enumerate(subs):
            P.op("dve", (lambda e, s_=s_, nq=nq, ob=ob: e.tensor_tensor(out=gsb[s_][0:nq, 0:ob], in0=P.ps[G_BANK][0:nq, 64 * s_:64 * s_ + ob],
                                                                         in1=gbias[0:nq, 0:ob], op=ALU.add)),
                 reads=[("ps", G_BANK), "gbias"], writes=[("gsb", s_)])
            P.op("dve", (lambda e, s_=s_, nq=nq, ob=ob: e.memset(gsb[s_][0:nq, ob:ob + 1], 1e30)), writes=[("gsb", s_)])
            P.op("dve", (lambda e, s_=s_, nq=nq: e.max(out=m8[s_][0:nq, :], in_=gsb[s_][0:nq, :])), reads=[("gsb", s_)], writes=[("m8", s_)])
            P.op("dve", (lambda e, s_=s_, nq=nq: e.tensor_scalar(out=negp[s_][0:nq, 64:128], in0=gsb[s_][0:nq, :], scalar1=m8[s_][0:nq, 3:4],
                                                                  scalar2=NEG, op0=ALU.is_lt, op1=ALU.mult)),
                 reads=[("gsb", s_), ("m8", s_)], writes=[("negp", s_)])

    def gate2(n):
        h, T, c0, Wq, sc, subs = items[n]
        qi = qa_of[n]
        qa = QA[qi]
        for s_, (o_, nq, ob) in enumerate(subs):
            P.op("pe", (lambda e, s_=s_, o_=o_, nq=nq: e.matmul(P.ps[T_BANK][:, o_:o_ + nq], lhsT=negp[s_][0:nq, :], rhs=ident[0:nq, 0:nq],
                                                                 start=True, stop=True)),
                 reads=[("negp", s_), "ident"], writes=[("ps", T_BANK)])
        P.op("dve", (lambda e: e.tensor_copy(out=qa[64:128, 0:Wq], in_=P.ps[T_BANK][64:128, 0:Wq])), reads=[("ps", T_BANK)], writes=[("QAn", qi)])

    pt_i = [0]

    def main(n):
        h, T, c0, Wq, sc, subs = items[n]
        i = h % 2
        qi = qa_of[n]
        qa = QA[qi]
        nk = 4 * T + 4
        obk = O_BANKS[n % 2]
        live = {}
        for step in range(nk + 2):
            if step < nk:
                kt = step
                sb_ = P.bank("S", S_BANKS)
                diag = kt >= 4 * T
                P.op("pe", (lambda e, kt=kt, sb_=sb_, diag=diag: e.matmul(P.ps[sb_][:, 0:Wq], lhsT=KA[i][:, 128 * kt:128 * kt + 128], rhs=qa[:, 0:Wq],
                                                                          start=True, stop=(not diag))),
                     reads=[("KAK", i), ("KAE", i), ("QAq", qi), ("QAn", qi)], writes=[("ps", sb_)])
                if diag:
                    P.op("pe", (lambda e, kt=kt, sb_=sb_: e.matmul(P.ps[sb_][:, 0:Wq], lhsT=ident[:, :], rhs=TRI[kt - 4 * T][:, c0:c0 + Wq],
                                                                   start=False, stop=True)),
                         reads=["ident", ("TRI", kt - 4 * T)], writes=[("ps", sb_)])
                pi = pt_i[0] % 4
                pt_i[0] += 1
                P.op("act", (lambda e, pi=pi, sb_=sb_: e.activation(out=PT[pi][:, 0:Wq], in_=P.ps[sb_][:, 0:Wq], func=AF.Exp)),
                     reads=[("ps", sb_)], writes=[("PT", pi)])
                live[kt] = pi
            if step >= 2:
                kt = step - 2
                pi = live.pop(kt)
                P.op("pe", (lambda e, kt=kt, pi=pi: e.matmul(P.ps[obk][:, 0:Wq], lhsT=VA[i][:, kt, :], rhs=PT[pi][:, 0:Wq],
                                                              start=(kt == 0), stop=(kt == nk - 1))),
                     reads=[("VAV", i, kt // vch), ("VA1", i), ("PT", pi)], writes=[("ps", obk)])
            if step == 0:
                qload(n + 2)
            if step == 1 and n + 1 < len(items):
                gate1(n + 1)
            if step == max(nk - 1, 2) and n + 1 < len(items):
                gate2(n + 1)
        r = n % 2
        P.op("dve", (lambda e: e.reciprocal(out=rden[r][:, 0:Wq], in_=P.ps[obk][64:128, 0:Wq])), reads=[("ps", obk)], writes=[("rden", r)])
        P.op("dve", (lambda e: e.tensor_tensor(out=OTb[r][:, 0:Wq], in0=P.ps[obk][0:64, 0:Wq], in1=rden[r][:, 0:Wq], op=ALU.mult)),
             reads=[("ps", obk), ("rden", r)], writes=[("OTb", r)])
        P.op("sp", (lambda e: e.dma_start(out=OTs[64 * h:64 * h + 64, sc + c0:sc + c0 + Wq], in_=OTb[r][:, 0:Wq])), reads=[("OTb", r)], dma="ot%d" % r)

    load_head(0)
    qload(0)
    qload(1)
    gate1(0)
    gate2(0)
    for n in range(len(items)):
        h = items[n][0]
        if (n == 0 or items[n - 1][0] != h) and h % 6 == 0 and h > 0:
            P.mid_barrier()
        if (n == 0 or items[n - 1][0] != h) and h + 1 < n_heads:
            load_head(h + 1)
        main(n)
    P.phase_end()

    P.phase_begin()
    P.ring_init()
    P.consts()
    P.ffn_setup()
    gains = P.sb("gains_sb", [128, 5, 8], F32)
    wcf = P.sb("wcf_sb", [128, 3, 44], F32)
    P.op("sp", lambda e: e.dma_start(out=gains[:], in_=gains_d[:, :, :]), writes=["gains"], dma="c0")
    P.op("sp", lambda e: e.dma_start(out=wcf[:], in_=wcf1_d[:, :, :]), writes=["wc"], dma="c2")
    xts = [P.sb("xt%d" % i, [128, 8, 512], F32) for i in range(2)]
    ots = [P.sb("ot%d" % i, [128, 8, 512], BF16) for i in range(2)]
    yo = [P.sb("yo%d" % i, [128, 8, 512], F32) for i in range(2)]
    hT = P.sb("hT", [128, 8, 512], BF16)
    ctiles = []
    for T in sorted(slot_of):
        if T in out_of:
            ctiles.append((slot_of[T] * 512, 512))
        else:
            ctiles.append((slot_of[T] * 512 + 512 - HALO, HALO))
    cout = [out_of.get(T) for T in sorted(slot_of)]
    OTv = OTs.rearrange("(c p) t -> p c t", p=128)
    for ti, (off, W) in enumerate(ctiles):
        P.ring_plan(Prog.sq_blocks(w_o, "wo") + Prog.ffn_blocks(w_up1, w_down1))
    for ti, (off, W) in enumerate(ctiles):
        xt = xts[ti % 2]
        xn = "xt%d" % (ti % 2)
        ot = ots[ti % 2]
        on = "ot%d" % (ti % 2)
        def cload(tj):
            if tj >= len(ctiles):
                return
            off_, W_ = ctiles[tj]
            P.op("sp", (lambda e: e.dma_start(out=xts[tj % 2][:, :, 0:W_], in_=x1s[:, :, off_:off_ + W_])),
                 writes=[("xt%d" % (tj % 2), c) for c in range(8)], dma="xin%d" % (tj % 2))
            P.op("sp", (lambda e: e.dma_start(out=ots[tj % 2][:, :, 0:W_], in_=OTv[:, :, off_:off_ + W_])),
                 writes=[("ot%d" % (tj % 2), c) for c in range(8)], dma="oin%d" % (tj % 2))
        if ti == 0:
            cload(0)
        cload(ti + 1)
        P.proj_add(xt, xn, ot, on, W, "wo", [6, 7])
        P.rmsnorm_stats_pre(W, [0, 1, 2, 3])
        P.rmsnorm_apply(xt, xn, W, gains[:, 3, :], hT, "hT")
        P.ffn(xt, xn, hT, W, wcf)
        if cout[ti] is None:
            continue
        m0 = cout[ti] * 512
        y = yo[ti % 2]
        yn = "yo%d" % (ti % 2)
        P.rmsnorm_stats_pre(W, [0, 1, 2, 3])
        P.rmsnorm_apply(xt, xn, W, gains[:, 4, :], y, yn)
        P.op("sp", (lambda e, y=y, m0=m0: e.dma_start(out=outT[:, :, m0:m0 + 512], in_=y[:, :, :])),
             reads=[(yn, c) for c in range(8)], dma="yout%d" % (ti % 2))
    assert P.cons == len(P.blocks)
    P.phase_end(last=True)
    return nc


def _lay_cols(v, n):
    return np.ascontiguousarray(v.reshape(n, 128).T)


def _fm(a):
    n = a.shape[0]
    return np.ascontiguousarray(a.T.reshape(8, 128, n).transpose(1, 0, 2))


_CACHE = {}


def _prog(name, fn):
    if name not in _CACHE:
        _CACHE[name] = fn()
    return _CACHE[name]


def kernel_unfused(x, mix_norm, sc_w_in, sc_w_conv, sc_w_out, moba_w_qkv, moba_w_o,
           ffn_norm, ffn_w_up, ffn_w_conv, ffn_w_down, final_norm):
    f32 = np.float32
    x = np.asarray(x, f32)
    cores = list(range(NCORE))
    gainsA = np.ascontiguousarray(np.stack([_lay_cols(np.asarray(mix_norm[0], f32), 8), _lay_cols(np.asarray(ffn_norm[0], f32), 8),
                                            _lay_cols(np.asarray(mix_norm[1], f32), 8)], axis=1))
    wcm = np.ascontiguousarray(np.stack([_lay_cols(np.asarray(sc_w_conv[0][j], f32), 8) for j in range(3)], axis=1))
    wcf0 = np.ascontiguousarray(np.stack([_lay_cols(np.asarray(ffn_w_conv[0][j], f32), 44) for j in range(3)], axis=1))
    wcf1 = np.ascontiguousarray(np.stack([_lay_cols(np.asarray(ffn_w_conv[1][j], f32), 44) for j in range(3)], axis=1))
    in_maps = []
    for c in cores:
        b, i = divmod(c, 4)
        xs = np.zeros((NT, D), f32)
        lo = i * TOK - HALO
        if lo < 0:
            xs[HALO:] = x[b, 0:TOK]
        else:
            xs[:] = x[b, lo:lo + NT]
        in_maps.append(dict(xT=_fm(xs), pos=(i * TOK + np.arange(TOK, dtype=f32))[None, :], gains=gainsA, wcm=wcm, wcf=wcf0,
                            w_in=np.asarray(sc_w_in[0], f32), w_out=np.asarray(sc_w_out[0], f32),
                            w_up=np.asarray(ffn_w_up[0], f32), w_down=np.asarray(ffn_w_down[0], f32),
                            w_qkv=np.asarray(moba_w_qkv[0], f32)))
    resA = run_bass_kernel_spmd(_prog("A", build_A), in_maps, core_ids=cores).results
    in_maps = []
    for c in cores:
        b, g = divmod(c, 4)
        rows = slice(256 * g, 256 * g + 256)
        QTh = np.concatenate([resA[4 * b + i]["QT"][rows] for i in range(4)], axis=1).reshape(4, 64, SEQ)
        KTh = np.concatenate([resA[4 * b + i]["KT"][rows] for i in range(4)], axis=1).reshape(4, 64, SEQ)
        Vb = np.concatenate([resA[4 * b + i]["V"][:, rows] for i in range(4)], axis=0)
        Vh = np.ascontiguousarray(Vb.reshape(SEQ, 4, 64).transpose(1, 0, 2))
        in_maps.append(dict(QTh=np.ascontiguousarray(QTh), KTh=np.ascontiguousarray(KTh), Vh=Vh))
    resB = run_bass_kernel_spmd(_prog("B", build_B), in_maps, core_ids=cores).results
    gainsC = np.ascontiguousarray(np.stack([_lay_cols(np.asarray(ffn_norm[1], f32), 8), _lay_cols(np.asarray(final_norm, f32), 8)], axis=1))
    in_maps = []
    for c in cores:
        b, i = divmod(c, 4)
        OTb = np.concatenate([resB[4 * b + g]["OTh"].reshape(256, SEQ) for g in range(4)], axis=0)
        ot = np.zeros((D, NT), OTb.dtype)
        lo = i * TOK - HALO
        if lo < 0:
            ot[:, HALO:] = OTb[:, 0:TOK]
        else:
            ot[:] = OTb[:, lo:lo + NT]
        OTt = np.ascontiguousarray(ot.reshape(8, 128, NT).transpose(1, 0, 2))
        in_maps.append(dict(x1T=resA[c]["x1T"], OTt=OTt, gains=gainsC, wcf=wcf1, w_o=np.asarray(moba_w_o[0], f32),
                            w_up=np.asarray(ffn_w_up[1], f32), w_down=np.asarray(ffn_w_down[1], f32)))
    resC = run_bass_kernel_spmd(_prog("C", build_C), in_maps, core_ids=cores).results
    out = np.empty((BATCH, SEQ, D), f32)
    for c in cores:
        b, i = divmod(c, 4)
        out[b, i * TOK:(i + 1) * TOK] = resC[c]["outT"].transpose(1, 0, 2).reshape(D, TOK).T
    return out


def kernel(x, mix_norm, sc_w_in, sc_w_conv, sc_w_out, moba_w_qkv, moba_w_o,
           ffn_norm, ffn_w_up, ffn_w_conv, ffn_w_down, final_norm):
    f32 = np.float32
    x = np.asarray(x, f32)
    cores = list(range(NCORE))
    lay = _lay_cols
    gains = np.ascontiguousarray(np.stack([lay(np.asarray(mix_norm[0], f32), 8), lay(np.asarray(ffn_norm[0], f32), 8),
                                           lay(np.asarray(mix_norm[1], f32), 8), lay(np.asarray(ffn_norm[1], f32), 8),
                                           lay(np.asarray(final_norm, f32), 8)], axis=1))
    wcm = np.ascontiguousarray(np.stack([lay(np.asarray(sc_w_conv[0][j], f32), 8) for j in range(3)], axis=1))
    wcf0 = np.ascontiguousarray(np.stack([lay(np.asarray(ffn_w_conv[0][j], f32), 44) for j in range(3)], axis=1))
    wcf1 = np.ascontiguousarray(np.stack([lay(np.asarray(ffn_w_conv[1][j], f32), 44) for j in range(3)], axis=1))
    shared = dict(gains=gains, wcm=wcm, wcf0=wcf0, wcf1=wcf1,
                  w_in=np.asarray(sc_w_in[0], f32), w_out=np.asarray(sc_w_out[0], f32),
                  w_up0=np.asarray(ffn_w_up[0], f32), w_down0=np.asarray(ffn_w_down[0], f32),
                  w_qkv=np.asarray(moba_w_qkv[0], f32), w_o=np.asarray(moba_w_o[0], f32),
                  w_up1=np.asarray(ffn_w_up[1], f32), w_down1=np.asarray(ffn_w_down[1], f32))
    SEG = 2048
    in_maps = []
    for c in cores:
        b, j = divmod(c, 4)
        npad = (3 - j) * SEG
        nreal = SEQ - npad
        xs = np.zeros((SEQ, D), f32)
        xs[npad:] = x[b, 0:nreal]
        pos = np.zeros((1, SEQ), f32)
        pos[0, npad:] = np.arange(nreal, dtype=f32)
        gb = np.zeros((128, 64), f32)
        gb[:, :npad // 256] = -2e30
        m = dict(shared)
        m.update(xT=_fm(xs), pos=pos, gbias=gb)
        in_maps.append(m)
    res = run_bass_kernel_spmd(_prog("F", build_fused), in_maps, core_ids=cores).results
    out = np.empty((BATCH, SEQ, D), f32)
    for c in cores:
        b, j = divmod(c, 4)
        o = res[c]["outT"].transpose(1, 0, 2).reshape(D, TOK).T
        out[b, j * SEG:(j + 1) * SEG] = o[0:SEG]
        out[b, (4 + j) * SEG:(5 + j) * SEG] = o[SEG:2 * SEG]
    return out
```

```python
import numpy as np
from contextlib import ExitStack
import concourse.bass as bass
import concourse.mybir as mybir
from concourse.bass_utils import run_bass_kernel_spmd
import ml_dtypes

F32 = mybir.dt.float32
BF16 = mybir.dt.bfloat16
AF = mybir.ActivationFunctionType
ALU = mybir.AluOpType

D = 1024
DFF = 2816
NH = 16
HD = 64
SEQ = 16384
BATCH = 2
NCORE = 8
TOK = 4096
HALO = 6
NT = TOK + HALO
TILES = [(0, HALO)] + [(HALO + 512 * i, 512) for i in range(8)]
EPS = 1e-6
NEG = -30000.0
ENGS = ["pe", "act", "dve", "pool", "sp"]


class Sched:
    def __init__(self, nc, stack):
        self.nc = nc
        self.stack = stack
        self.ops = {e: [] for e in ENGS}
        self.esem = {e: stack.enter_context(nc.semaphore("se_" + e)) for e in ENGS}
        self.ecount = {e: 0 for e in ENGS}
        self.epoch = 0
        self.waited = {}
        self.lastw = {}
        self.readers = {}
        self.dsem = {}

    def _dstream(self, name):
        if name not in self.dsem:
            sem = self.stack.enter_context(self.nc.semaphore("sd_%d" % len(self.dsem)))
            self.dsem[name] = [sem, 0]
        return self.dsem[name]

    def op(self, eng, fn, reads=(), writes=(), dma=None, multi=False):
        deps = []
        for b in reads:
            t = self.lastw.get(b)
            if t is not None:
                deps.append(t)
            if isinstance(b, tuple) and b[0] == "ps":
                deps.extend(r for r in self.readers.get(b, ()) if r[3] != eng)
        for b in writes:
            t = self.lastw.get(b)
            if t is not None:
                deps.append(t)
            deps.extend(self.readers.get(b, ()))
        need = {}
        for (key, sem, val, src) in deps:
            if eng == "pe" and src == "pe":
                continue
            if key not in need or need[key][1] < val:
                need[key] = (sem, val)
        waits = []
        for key, (sem, val) in need.items():
            k = (eng, key)
            if self.waited.get(k, 0) >= val:
                continue
            self.waited[k] = val
            waits.append((sem, val))
        if dma is not None:
            ent = self._dstream(dma)
            ent[1] += 16
            tok = (("d", dma), ent[0], ent[1], "dma")
            inc = 16
        else:
            self.ecount[eng] += 1
            tok = (("e", eng, self.epoch), self.esem[eng], self.ecount[eng], eng)
            inc = 1
        self.ops[eng].append((waits if not multi else [("multi",)] + waits, fn, tok[1], inc))
        for b in reads:
            self.readers.setdefault(b, []).append(tok)
        for b in writes:
            self.lastw[b] = tok
            self.readers[b] = []
        return tok

    def barrier(self, scratch_ap):
        final = [(ent[0], ent[1]) for ent in self.dsem.values() if ent[1] > 0]
        final += [(self.esem[en], self.ecount[en]) for en in ENGS if en != "sp" and self.ecount[en] > 0]
        ent = self._dstream("__barrier__")
        ent[1] += 16
        dst, src = scratch_ap
        self.ops["sp"].append((final, (lambda e: e.dma_start(out=dst, in_=src)), ent[0], 16))
        for en in ENGS:
            self.ops[en].append(([(ent[0], ent[1])], None, None, 0))
        self.lastw.clear()
        self.readers.clear()

    def new_epoch(self):
        self.epoch += 1
        self.esem = {e: self.stack.enter_context(self.nc.semaphore("se%d_%s" % (self.epoch, e))) for e in ENGS}
        self.ecount = {e: 0 for e in ENGS}
        self.waited = {}

    def emit(self, last=True):
        nc = self.nc
        final = [(ent[0], ent[1]) for ent in self.dsem.values() if ent[1] > 0]

        def replay(lst, e, last=False, attach=True):
            for waits, fn, sem, inc in lst:
                multi = bool(waits) and waits[0] == ("multi",)
                if multi:
                    waits = waits[1:]
                if fn is None or not attach or not waits or multi:
                    for (s, v) in waits:
                        e.wait_ge(s, v)
                    if fn is not None:
                        fn(e).then_inc(sem, inc)
                else:
                    for (s, v) in waits[:-1]:
                        e.wait_ge(s, v)
                    ins = fn(e)
                    ins._wait_ge(*waits[-1])
                    ins.then_inc(sem, inc)
            if last:
                for (s, v) in final:
                    e.wait_ge(s, v)
                for en in ENGS:
                    if self.ecount[en] > 0:
                        e.wait_ge(self.esem[en], self.ecount[en])

        with nc.Block() as block:
            @block.tensor
            def _(e):
                replay(self.ops["pe"], e, attach=False)

            @block.scalar
            def _(e):
                replay(self.ops["act"], e)

            @block.vector
            def _(e):
                replay(self.ops["dve"], e)

            @block.gpsimd
            def _(e):
                replay(self.ops["pool"], e)

            @block.sync
            def _(e):
                replay(self.ops["sp"], e, last=last)
        self.ops = {e: [] for e in ENGS}


class Prog:
    RING = 4
    SLOT = 6144

    def __init__(self):
        self.nc = bass.Bass("TRN2", target_bir_lowering=False)
        self.stack = ExitStack()
        self.S = Sched(self.nc, self.stack)
        self.ps = [self.stack.enter_context(self.nc.psum_tensor("ps%d" % i, [128, 512], F32)) for i in range(8)]
        self.ps_rr = {}
        self.blocks = []
        self.issued = 0
        self.cons = 0
        self.released = 0
        self.slots = None
        self.uid = 0
        self.pstack = None
        self.phase_id = 0
        self.use_wscr = False
        self.wscr = {}
        bar = self.nc.dram_tensor("bar_scratch", [2, 64], F32, kind="Internal").ap()
        self.bar_dst, self.bar_src = bar[0:1, :], bar[1:2, :]

    def dram(self, name, shape, dtype, kind):
        return self.nc.dram_tensor(name, list(shape), dtype, kind=kind).ap()

    def sb(self, name, shape, dtype):
        st = self.pstack if self.pstack is not None else self.stack
        self.uid += 1
        return st.enter_context(self.nc.sbuf_tensor("%s_u%d" % (name, self.uid), list(shape), dtype))

    def phase_begin(self):
        self.phase_id += 1
        self.pstack = ExitStack()
        self.blocks = []
        self.issued = self.cons = self.released = 0
        self.ps_rr = {}

    def phase_end(self, last=False):
        if not last:
            self.S.barrier((self.bar_dst, self.bar_src))
        self.S.emit(last=last)
        if not last:
            self.S.new_epoch()
        self.pstack.close()
        self.pstack = None

    def mid_barrier(self):
        self.S.barrier((self.bar_dst, self.bar_src))
        self.S.emit(last=False)
        self.S.new_epoch()

    def bank(self, pool, banks):
        i = self.ps_rr.get(pool, 0)
        self.ps_rr[pool] = i + 1
        return banks[i % len(banks)]

    def op(self, *a, **k):
        return self.S.op(*a, **k)

    def ring_init(self):
        self.slots = [self.sb("wslot%d" % i, [128, self.SLOT], BF16) for i in range(self.RING)]

    def ring_plan(self, blocks):
        self.blocks.extend(blocks)

    def _issue(self, i):
        tag, segs = self.blocks[i]
        s = i % self.RING
        slot = self.slots[s]
        key = (self.phase_id, tag)
        total = segs[0][1] * segs[0][3]
        names = [("w", s, si) for si in range(len(segs))]
        if self.use_wscr and key in self.wscr:
            scr = self.wscr[key]
            self.op("pool", (lambda e: e.dma_start(out=slot[:, 0:total], in_=scr[:, 0:total])),
                    reads=[("wscr", key)], writes=names, dma="w%d_0" % s)
            return
        for si, (off, kc, n, stride, src) in enumerate(segs):
            dst = slot[:, 0:kc * stride].rearrange("p (k n) -> p k n", n=stride)[:, :, off:off + n]
            self.op("pool", (lambda e, dst=dst, src=src: e.dma_start(out=dst, in_=src)),
                    writes=[("w", s, si)], dma="w%d_%d" % (s, si))
        if self.use_wscr:
            scr = self.dram("wscr_p%d_%s" % key, [128, self.SLOT], BF16, "Internal")
            self.wscr[key] = scr
            self.op("pool", (lambda e: e.dma_start(out=scr[:, 0:total], in_=slot[:, 0:total])),
                    reads=names, writes=[("wscr", key)], dma="ws%d" % s)

    def _pump(self):
        while self.issued < len(self.blocks) and self.issued < self.released + self.RING:
            self._issue(self.issued)
            self.issued += 1

    def ring_next(self, tag):
        i = self.cons
        self.cons += 1
        assert self.blocks[i][0] == tag, (self.blocks[i][0], tag)
        self._pump()
        assert self.issued > i, "ring: too many live blocks"
        s = i % self.RING
        nseg = len(self.blocks[i][1])
        return self.slots[s], [("w", s, si) for si in range(nseg)]

    def ring_done(self, n=1):
        self.released += n
        self._pump()

    def consts(self):
        self.ones = self.sb("ones_bf", [128, 128], BF16)
        self.op("dve", lambda e: e.memset(self.ones[:], 1.0), writes=["ones"])
        self.sq = [self.sb("sq%d" % i, [128, 512], BF16) for i in range(4)]
        self.sq_i = 0
        self.rsb = self.sb("rsb", [128, 512], F32)
        self.rstd = self.sb("rstd", [128, 512], F32)
        self.rscr = self.sb("rscr", [128, 512], F32)

    def rmsnorm_stats(self, xt, xname, W, banks):
        bk = self.bank("A", banks)
        ps = self.ps[bk]
        for c in range(8):
            i = self.sq_i % 4
            self.sq_i += 1
            sq = self.sq[i]
            self.op("act", (lambda e, sq=sq, c=c: e.activation(out=sq[:, 0:W], in_=xt[:, c, 0:W], func=AF.Square)),
                    reads=[(xname, c)], writes=[("sq", i)])
            self.op("pe", (lambda e, sq=sq, c=c: e.matmul(ps[:, 0:W], lhsT=self.ones[:], rhs=sq[:, 0:W],
                                                          start=(c == 0), stop=(c == 7))),
                    reads=[("sq", i), "ones"], writes=[("ps", bk)])
        self.op("dve", (lambda e: e.tensor_scalar(out=self.rsb[:, 0:W], in0=ps[:, 0:W], scalar1=1.0 / D, scalar2=EPS,
                                                  op0=ALU.mult, op1=ALU.add)),
                reads=[("ps", bk)], writes=["rsb"])
        self.op("act", (lambda e: e.activation(out=self.rsb[:, 0:W], in_=self.rsb[:, 0:W], func=AF.Sqrt)),
                reads=["rsb"], writes=["rsb"])
        self.op("dve", (lambda e: e.reciprocal(out=self.rstd[:, 0:W], in_=self.rsb[:, 0:W])),
                reads=["rsb"], writes=["rstd"])

    def rmsnorm_apply(self, xt, xname, W, gcol, out, oname):
        for c in range(8):
            self.op("dve", (lambda e, c=c: e.scalar_tensor_tensor(out=out[:, c, 0:W], in0=xt[:, c, 0:W],
                                                                    scalar=gcol[:, c:c + 1], in1=self.rstd[:, 0:W],
                                                                    op0=ALU.mult, op1=ALU.mult)),
                    reads=[(xname, c), "rstd", "gains"], writes=[(oname, c)])

    def conv3(self, buf, bname, t, tname, W, wc, ci, hs, hsname):
        self.op("act", (lambda e: e.activation(out=buf[:, 0:2], in_=hs[:, ci, :], func=AF.Copy)),
                reads=[(hsname, ci)], writes=[(bname, "h")])
        self.op("act", (lambda e: e.activation(out=t[:, 0:W], in_=buf[:, 0:W], func=AF.Copy, scale=wc[:, 0, ci:ci + 1])),
                reads=[(bname, "h"), (bname, "m"), "wc"], writes=[tname])
        self.op("dve", (lambda e: e.scalar_tensor_tensor(out=t[:, 0:W], in0=buf[:, 1:1 + W], scalar=wc[:, 1, ci:ci + 1],
                                                         in1=t[:, 0:W], op0=ALU.mult, op1=ALU.add)),
                reads=[(bname, "h"), (bname, "m"), tname, "wc"], writes=[tname])
        self.op("dve", (lambda e: e.scalar_tensor_tensor(out=t[:, 0:W], in0=buf[:, 2:2 + W], scalar=wc[:, 2, ci:ci + 1],
                                                         in1=t[:, 0:W], op0=ALU.mult, op1=ALU.add)),
                reads=[(bname, "m"), tname, "wc"], writes=[tname])
        self.op("act", (lambda e: e.activation(out=hs[:, ci, :], in_=buf[:, W:W + 2], func=AF.Copy)),
                reads=[(bname, "m"), (bname, "h")], writes=[(hsname, ci)])

    def ffn_setup(self):
        self.actT = self.sb("actT", [128, 22, 512], BF16)
        self.gb = [self.sb("gb%d" % i, [128, 514], F32) for i in range(2)]
        self.ub = [self.sb("ub%d" % i, [128, 514], F32) for i in range(2)]
        self.tg = [self.sb("tg%d" % i, [128, 512], F32) for i in range(2)]
        self.tu = [self.sb("tu%d" % i, [128, 512], F32) for i in range(2)]
        self.hsf = self.sb("hsf", [128, 44, 2], F32)
        self.op("dve", lambda e: e.memset(self.hsf[:], 0.0), writes=[("hsf", i) for i in range(44)])
        self.pair_i = 0

    @staticmethod
    def ffn_blocks(w_up, w_down):
        bl = []
        for b in range(11):
            g = w_up[:, 256 * b:256 * b + 256].rearrange("(k p) n -> p k n", p=128)
            u = w_up[:, DFF + 256 * b:DFF + 256 * b + 256].rearrange("(k p) n -> p k n", p=128)
            bl.append(("up%d" % b, [(0, 8, 256, 512, g), (256, 8, 256, 512, u)]))
        for dh in range(2):
            for kh in range(2):
                src = w_down[kh * 1408:(kh + 1) * 1408, dh * 512:(dh + 1) * 512].rearrange("(k p) n -> p k n", p=128)
                bl.append(("dn%d%d" % (dh, kh), [(0, 11, 512, 512, src)]))
        return bl

    def ffn(self, xt, xname, hT, W, wcf, hook=None):
        pend = None
        for b in range(11):
            slot, wnames = self.ring_next("up%d" % b)
            for s in range(2):
                fc = 2 * b + s
                pi = self.pair_i % 2
                self.pair_i += 1
                bg = self.bank("A", [0, 1, 2, 3])
                bu = self.bank("A", [0, 1, 2, 3])
                for (bk, off) in ((bg, 0), (bu, 256)):
                    for kc in range(8):
                        lw = slot[:, kc * 512 + off + s * 128: kc * 512 + off + s * 128 + 128]
                        self.op("pe", (lambda e, bk=bk, lw=lw, kc=kc: e.matmul(self.ps[bk][:, 0:W], lhsT=lw, rhs=hT[:, kc, 0:W],
                                                                                start=(kc == 0), stop=(kc == 7))),
                                reads=wnames + [("hT", kc)], writes=[("ps", bk)])
                gb, ub, tg, tu = self.gb[pi], self.ub[pi], self.tg[pi], self.tu[pi]
                self.op("act", (lambda e, gb=gb, bg=bg: e.activation(out=gb[:, 2:2 + W], in_=self.ps[bg][:, 0:W], func=AF.Copy)),
                        reads=[("ps", bg)], writes=[("gb%d" % pi, "m")])
                self.op("act", (lambda e, ub=ub, bu=bu: e.activation(out=ub[:, 2:2 + W], in_=self.ps[bu][:, 0:W], func=AF.Copy)),
                        reads=[("ps", bu)], writes=[("ub%d" % pi, "m")])
                self.conv3(gb, "gb%d" % pi, tg, "tg%d" % pi, W, wcf, fc, self.hsf, "hsf")
                self.conv3(ub, "ub%d" % pi, tu, "tu%d" % pi, W, wcf, 22 + fc, self.hsf, "hsf")
                if pend is not None:
                    pend()
                if hook:
                    hook.pop(0)()

                def stage2(tg=tg, tu=tu, pi=pi, fc=fc):
                    self.op("act", (lambda e: e.activation(out=tg[:, 0:W], in_=tg[:, 0:W], func=AF.Silu)),
                            reads=["tg%d" % pi], writes=["tg%d" % pi])
                    self.op("dve", (lambda e: e.tensor_tensor(out=self.actT[:, fc, 0:W], in0=tg[:, 0:W], in1=tu[:, 0:W], op=ALU.mult)),
                            reads=["tg%d" % pi, "tu%d" % pi], writes=[("actT", fc)])
                pend = stage2
            self.ring_done()
        pend()
        while hook:
            hook.pop(0)()
        import os
        if os.environ.get("DBG_FFN") == "1":
            self.cons += 4
            self.released += 4
            return
        for dh in range(2):
            banks = [4, 5, 6, 7] if dh == 0 else [0, 1, 2, 3]
            sl = [self.ring_next("dn%d%d" % (dh, kh)) for kh in range(2)]
            for o in range(4):
                for kh in range(2):
                    slot, wnames = sl[kh]
                    for kc in range(11):
                        lw = slot[:, kc * 512 + o * 128: kc * 512 + o * 128 + 128]
                        fc = kh * 11 + kc
                        self.op("pe", (lambda e, lw=lw, o=o, fc=fc, kh=kh, kc=kc, banks=banks: e.matmul(
                            self.ps[banks[o]][:, 0:W], lhsT=lw, rhs=self.actT[:, fc, 0:W],
                            start=(kh == 0 and kc == 0), stop=(kh == 1 and kc == 10))),
                            reads=wnames + [("actT", fc)], writes=[("ps", banks[o])])
            self.ring_done(2)
            for o in range(4):
                c = dh * 4 + o
                self.op("dve", (lambda e, c=c, o=o, banks=banks: e.tensor_tensor(out=xt[:, c, 0:W], in0=xt[:, c, 0:W],
                                                                                 in1=self.ps[banks[o]][:, 0:W], op=ALU.add)),
                        reads=[(xname, c), ("ps", banks[o])], writes=[(xname, c)])

    @staticmethod
    def sq_blocks(w, tag):
        bl = []
        for ob in range(2):
            src = w[:, 512 * ob:512 * ob + 512].rearrange("(k p) n -> p k n", p=128)
            bl.append(("%s%d" % (tag, ob), [(0, 8, 512, 512, src)]))
        return bl

    def proj_add(self, xt, xname, rT, rname, W, tag, banks):
        for ob in range(2):
            slot, wnames = self.ring_next("%s%d" % (tag, ob))
            for o in range(4):
                bk = self.bank("B", banks)
                for kc in range(8):
                    lw = slot[:, kc * 512 + o * 128: kc * 512 + o * 128 + 128]
                    self.op("pe", (lambda e, bk=bk, lw=lw, kc=kc: e.matmul(self.ps[bk][:, 0:W], lhsT=lw, rhs=rT[:, kc, 0:W],
                                                                            start=(kc == 0), stop=(kc == 7))),
                            reads=wnames + [(rname, kc)], writes=[("ps", bk)])
                c = ob * 4 + o
                self.op("dve", (lambda e, c=c, bk=bk: e.tensor_tensor(out=xt[:, c, 0:W], in0=xt[:, c, 0:W],
                                                                       in1=self.ps[bk][:, 0:W], op=ALU.add)),
                        reads=[(xname, c), ("ps", bk)], writes=[(xname, c)])
            self.ring_done()


def build_A():
    P = Prog()
    nc = P.nc
    xT = P.dram("xT", [128, 8, NT], F32, "ExternalInput")
    pos = P.dram("pos", [1, TOK], F32, "ExternalInput")
    gains_d = P.dram("gains", [128, 3, 8], F32, "ExternalInput")
    wcm_d = P.dram("wcm", [128, 3, 8], F32, "ExternalInput")
    wcf_d = P.dram("wcf", [128, 3, 44], F32, "ExternalInput")
    w_in = P.dram("w_in", [D, 3 * D], F32, "ExternalInput")
    w_out = P.dram("w_out", [D, D], F32, "ExternalInput")
    w_up = P.dram("w_up", [D, 2 * DFF], F32, "ExternalInput")
    w_down = P.dram("w_down", [DFF, D], F32, "ExternalInput")
    w_qkv = P.dram("w_qkv", [D, 3 * D], F32, "ExternalInput")
    x1T = P.dram("x1T", [128, 8, NT], F32, "ExternalOutput")
    QT = P.dram("QT", [D, TOK], BF16, "ExternalOutput")
    KT = P.dram("KT", [D, TOK], BF16, "ExternalOutput")
    Vo = P.dram("V", [TOK, D], BF16, "ExternalOutput")

    P.ring_init()
    P.consts()
    P.ffn_setup()
    gains = P.sb("gains_sb", [128, 3, 8], F32)
    wcm = P.sb("wcm_sb", [128, 3, 8], F32)
    wcf = P.sb("wcf_sb", [128, 3, 44], F32)
    P.op("sp", lambda e: e.dma_start(out=gains[:], in_=gains_d[:, :, :]), writes=["gains"], dma="c0")
    P.op("sp", lambda e: e.dma_start(out=wcm[:], in_=wcm_d[:, :, :]), writes=["wc"], dma="c1")
    P.op("sp", lambda e: e.dma_start(out=wcf[:], in_=wcf_d[:, :, :]), writes=["wc"], dma="c2")

    xts = [P.sb("xt%d" % i, [128, 8, 512], F32) for i in range(2)]
    hT = P.sb("hT", [128, 8, 512], BF16)
    yT = P.sb("yT", [128, 8, 512], BF16)
    cbuf = [P.sb("cbuf%d" % i, [128, 512], F32) for i in range(2)]
    cvb = [P.sb("cvb%d" % i, [128, 514], F32) for i in range(2)]
    tm = [P.sb("tm%d" % i, [128, 512], F32) for i in range(2)]
    hsm = P.sb("hsm", [128, 8, 2], F32)
    P.op("dve", lambda e: e.memset(hsm[:], 0.0), writes=[("hsm", i) for i in range(8)])

    inv_row = P.sb("inv_row", [1, 128], F32)
    half = HD // 2
    inv = (np.float32(10000.0) ** (-(np.arange(half, dtype=np.float32) / np.float32(half)))).astype(np.float32)
    for i in range(half):
        P.op("dve", (lambda e, i=i: e.memset(inv_row[0:1, i:128:32], float(inv[i]))), writes=["inv_row"])
    sgn = P.sb("sgn", [128, 1], F32)
    for q in range(4):
        P.op("dve", (lambda e, q=q: e.memset(sgn[32 * q:32 * q + 32, :], -1.0 if q % 2 == 0 else 1.0)), writes=["sgn"])
    pos_sb = P.sb("pos_sb", [1, TOK], F32)
    P.op("sp", lambda e: e.dma_start(out=pos_sb[:], in_=pos[:, :]), writes=["pos"], dma="c3")
    ang = P.sb("ang", [128, 512], F32)
    kf = P.sb("kf", [128, 512], F32)
    ki = P.sb("ki", [128, 512], mybir.dt.int32)
    cosT = P.sb("cosT", [128, 512], F32)
    sinT = P.sb("sinT", [128, 512], F32)
    rt1 = [P.sb("rt1_%d" % i, [128, 512], F32) for i in range(2)]
    rxs = [P.sb("rxs_%d" % i, [128, 512], F32) for i in range(2)]
    qkb = [P.sb("qkb%d" % i, [128, 512], BF16) for i in range(2)]
    vtok = [P.sb("vtok%d" % i, [128, 512], BF16) for i in range(2)]

    def mixer_blocks():
        bl = []
        for j in range(4):
            segs = []
            for si in range(3):
                src = w_in[:, si * D + 256 * j: si * D + 256 * j + 256].rearrange("(k p) n -> p k n", p=128)
                segs.append((256 * si, 8, 256, 768, src))
            bl.append(("in%d" % j, segs))
        return bl

    def qkv_blocks():
        bl = []
        for j in range(6):
            src = w_qkv[:, 512 * j:512 * j + 512].rearrange("(k p) n -> p k n", p=128)
            bl.append(("qkv%d" % j, [(0, 8, 512, 512, src)]))
        return bl

    for ti, (off, W) in enumerate(TILES):
        P.ring_plan(mixer_blocks() + Prog.sq_blocks(w_out, "wo") + Prog.ffn_blocks(w_up, w_down))
        if ti > 0:
            P.ring_plan(qkv_blocks())

    TWO_PI = 2.0 * np.pi
    C1 = float(np.float32(6.28125))
    C2 = float(np.float32(TWO_PI - 6.28125))

    rope_i = [0]
    tri_i = [0]
    import os
    dbg = int(os.environ.get("DBG_STOP", "99"))
    for ti, (off, W) in enumerate(TILES):
        xt = xts[ti % 2]
        xn = "xt%d" % (ti % 2)
        P.op("sp", (lambda e, xt=xt, off=off, W=W: e.dma_start(out=xt[:, :, 0:W], in_=xT[:, :, off:off + W])),
             writes=[(xn, c) for c in range(8)], dma="xin%d" % (ti % 2))
        if dbg == 0:
            P.op("sp", (lambda e, xt=xt, off=off, W=W: e.dma_start(out=x1T[:, :, off:off + W], in_=xt[:, :, 0:W])),
                 reads=[(xn, c) for c in range(8)], dma="xout%d" % (ti % 2))
            continue
        P.rmsnorm_stats(xt, xn, W, [0, 1, 2, 3, 4, 5])
        P.rmsnorm_apply(xt, xn, W, gains[:, 0, :], hT, "hT")
        if dbg == 1:
            P.op("sp", (lambda e, xt=xt, off=off, W=W: e.dma_start(out=x1T[:, :, off:off + W], in_=xt[:, :, 0:W])),
                 reads=[(xn, c) for c in range(8)] + [("hT", c) for c in range(8)], dma="xout%d" % (ti % 2))
            continue
        pend = None
        for j in range(4):
            slot, wn = P.ring_next("in%d" % j)
            for s in range(2):
                cj = 2 * j + s
                pi = tri_i[0] % 2
                tri_i[0] += 1
                bks = [P.bank("A", [0, 1, 2, 3, 4, 5]) for _ in range(3)]
                for si in (1, 2, 0):
                    bk = bks[si]
                    for kc in range(8):
                        lw = slot[:, kc * 768 + si * 256 + s * 128: kc * 768 + si * 256 + s * 128 + 128]
                        P.op("pe", (lambda e, bk=bk, lw=lw, kc=kc, W=W: e.matmul(P.ps[bk][:, 0:W], lhsT=lw, rhs=hT[:, kc, 0:W],
                                                                                 start=(kc == 0), stop=(kc == 7))),
                             reads=wn + [("hT", kc)], writes=[("ps", bk)])
                cb, cv, t = cbuf[pi], cvb[pi], tm[pi]
                P.op("act", (lambda e, cb=cb, bk=bks[1], W=W: e.activation(out=cb[:, 0:W], in_=P.ps[bk][:, 0:W], func=AF.Copy)),
                     reads=[("ps", bks[1])], writes=["cbuf%d" % pi])
                P.op("dve", (lambda e, cb=cb, cv=cv, bk=bks[2], W=W: e.tensor_tensor(out=cv[:, 2:2 + W], in0=cb[:, 0:W],
                                                                                      in1=P.ps[bk][:, 0:W], op=ALU.mult)),
                     reads=["cbuf%d" % pi, ("ps", bks[2])], writes=[("cvb%d" % pi, "m")])
                P.conv3(cv, "cvb%d" % pi, t, "tm%d" % pi, W, wcm, cj, hsm, "hsm")
                P.op("dve", (lambda e, t=t, cj=cj, bk=bks[0], W=W: e.tensor_tensor(out=yT[:, cj, 0:W], in0=t[:, 0:W],
                                                                                    in1=P.ps[bk][:, 0:W], op=ALU.mult)),
                     reads=["tm%d" % pi, ("ps", bks[0])], writes=[("yT", cj)])
            P.ring_done()
        if dbg == 2:
            P.op("sp", (lambda e, xt=xt, off=off, W=W: e.dma_start(out=x1T[:, :, off:off + W], in_=xt[:, :, 0:W])),
                 reads=[(xn, c) for c in range(8)] + [("yT", c) for c in range(8)], dma="xout%d" % (ti % 2))
            continue
        P.proj_add(xt, xn, yT, "yT", W, "wo", [6, 7])
        if dbg == 3:
            P.op("sp", (lambda e, xt=xt, off=off, W=W: e.dma_start(out=x1T[:, :, off:off + W], in_=xt[:, :, 0:W])),
                 reads=[(xn, c) for c in range(8)], dma="xout%d" % (ti % 2))
            continue
        P.rmsnorm_stats(xt, xn, W, [0, 1, 2, 3])
        P.rmsnorm_apply(xt, xn, W, gains[:, 1, :], hT, "hT")
        P.ffn(xt, xn, hT, W, wcf)
        P.op("sp", (lambda e, xt=xt, off=off, W=W: e.dma_start(out=x1T[:, :, off:off + W], in_=xt[:, :, 0:W])),
             reads=[(xn, c) for c in range(8)], dma="xout%d" % (ti % 2))
        if ti == 0 or dbg == 4:
            if ti > 0:
                P.cons += 6; P.released += 6
            continue
        m0 = off - HALO
        P.rmsnorm_stats(xt, xn, W, [0, 1, 2, 3])
        P.rmsnorm_apply(xt, xn, W, gains[:, 2, :], hT, "hT")
        bk = P.bank("A", [0, 1, 2, 3])
        P.op("pe", (lambda e, bk=bk, m0=m0: e.matmul(P.ps[bk][:, 0:512], lhsT=inv_row[:], rhs=pos_sb[0:1, m0:m0 + 512],
                                                      start=True, stop=True)),
             reads=["inv_row", "pos"], writes=[("ps", bk)])
        P.op("dve", (lambda e, bk=bk: e.tensor_copy(out=ang[:], in_=P.ps[bk][:, :])), reads=[("ps", bk)], writes=["ang"])
        for which, dst in (("sin", sinT), ("cos", cosT)):
            if which == "cos":
                P.op("dve", (lambda e: e.tensor_scalar(out=ang[:], in0=ang[:], scalar1=float(np.pi / 2), scalar2=None, op0=ALU.add)),
                     reads=["ang"], writes=["ang"])
            P.op("dve", (lambda e: e.tensor_scalar(out=ki[:], in0=ang[:], scalar1=float(1.0 / TWO_PI), scalar2=None, op0=ALU.mult)),
                 reads=["ang"], writes=["ki"])
            P.op("dve", (lambda e: e.tensor_copy(out=kf[:], in_=ki[:])), reads=["ki"], writes=["kf"])
            P.op("dve", (lambda e, dst=dst: e.scalar_tensor_tensor(out=dst[:], in0=kf[:], scalar=-C1, in1=ang[:], op0=ALU.mult, op1=ALU.add)),
                 reads=["kf", "ang"], writes=[which])
            P.op("dve", (lambda e, dst=dst: e.scalar_tensor_tensor(out=dst[:], in0=kf[:], scalar=-C2, in1=dst[:], op0=ALU.mult, op1=ALU.add)),
                 reads=["kf", which], writes=[which])
            P.op("dve", (lambda e, dst=dst: e.tensor_scalar(out=kf[:], in0=dst[:], scalar1=float(np.pi), scalar2=-TWO_PI, op0=ALU.is_gt, op1=ALU.mult)),
                 reads=[which], writes=["kf"])
            P.op("dve", (lambda e, dst=dst: e.tensor_tensor(out=dst[:], in0=dst[:], in1=kf[:], op=ALU.add)),
                 reads=[which, "kf"], writes=[which])
            P.op("dve", (lambda e, dst=dst: e.tensor_scalar(out=kf[:], in0=dst[:], scalar1=float(-np.pi), scalar2=TWO_PI, op0=ALU.is_lt, op1=ALU.mult)),
                 reads=[which], writes=["kf"])
            P.op("dve", (lambda e, dst=dst: e.tensor_tensor(out=dst[:], in0=dst[:], in1=kf[:], op=ALU.add)),
                 reads=[which, "kf"], writes=[which])
            if which == "sin":
                P.op("act", (lambda e: e.activation(out=sinT[:], in_=sinT[:], func=AF.Sin, scale=sgn[:, 0:1])),
                     reads=["sin", "sgn"], writes=["sin"])
            else:
                P.op("act", (lambda e: e.activation(out=cosT[:], in_=cosT[:], func=AF.Sin)), reads=["cos"], writes=["cos"])
        if dbg == 5:
            P.cons += 6; P.released += 6
            continue
        for j in range(4):
            slot, wn = P.ring_next("qkv%d" % j)
            scale = 0.125 if j < 2 else 1.0
            dstT = QT if j < 2 else KT
            for o in range(4):
                ch = (j % 2) * 4 + o
                ri = rope_i[0] % 2
                rope_i[0] += 1
                bk = P.bank("A", [0, 1, 2, 3])
                for kc in range(8):
                    lw = slot[:, kc * 512 + o * 128: kc * 512 + o * 128 + 128]
                    P.op("pe", (lambda e, bk=bk, lw=lw, kc=kc: e.matmul(P.ps[bk][:, :], lhsT=lw, rhs=hT[:, kc, :],
                                                                         start=(kc == 0), stop=(kc == 7))),
                         reads=wn + [("hT", kc)], writes=[("ps", bk)])
                t1, xs, ob = rt1[ri], rxs[ri], qkb[ri]
                dq = int(os.environ.get("DBG_QK", "9"))
                if dq >= 2:
                    for q in range(4):
                        src = 32 * (q ^ 1) if not os.environ.get("DBG_NOSHIFT") else 32 * q
                        P.op("act", (lambda e, xs=xs, q=q, src=src, bk=bk, scale=scale: e.activation(
                            out=xs[32 * q:32 * q + 32, :], in_=P.ps[bk][src:src + 32, :], func=AF.Copy, scale=scale)),
                             reads=[("ps", bk)], writes=[("rxs%d" % ri, q), ("psr", bk)])
                if dq >= 3:
                    P.op("dve", (lambda e, t1=t1, bk=bk, scale=scale: e.scalar_tensor_tensor(out=t1[:], in0=P.ps[bk][:, :], scalar=scale,
                                                                                              in1=cosT[:], op0=ALU.mult, op1=ALU.mult)),
                         reads=[("ps", bk), "cos", ("psr", bk)], writes=["rt1%d" % ri])
                if dq >= 4:
                    P.op("dve", (lambda e, xs=xs: e.tensor_tensor(out=xs[:], in0=xs[:], in1=sinT[:], op=ALU.mult)),
                         reads=[("rxs%d" % ri, q) for q in range(4)] + ["sin"], writes=[("rxs%d" % ri, q) for q in range(4)])
                if dq >= 5:
                    P.op("dve", (lambda e, t1=t1, xs=xs, ob=ob: e.tensor_tensor(out=ob[:], in0=t1[:], in1=xs[:], op=ALU.add)),
                         reads=["rt1%d" % ri] + [("rxs%d" % ri, q) for q in range(4)], writes=["qkb%d" % ri])
                else:
                    P.op("dve", (lambda e, ob=ob, bk=bk: e.tensor_copy(out=ob[:], in_=P.ps[bk][:, :])),
                         reads=[("ps", bk)], writes=["qkb%d" % ri])
                if not os.environ.get("DBG_NOSTORE"):
                    P.op("sp", (lambda e, ob=ob, dstT=dstT, ch=ch, m0=m0: e.dma_start(out=dstT[128 * ch:128 * ch + 128, m0:m0 + 512], in_=ob[:])),
                         reads=["qkb%d" % ri], dma="qk%d" % ri)
            P.ring_done()
        if dbg == 6:
            P.cons += 2; P.released += 2
            continue
        vi = 0
        slots_v = [P.ring_next("qkv4"), P.ring_next("qkv5")]
        for ts in range(4):
            for hf in range(2):
                slot, wn = slots_v[hf]
                bk = P.bank("A", [0, 1, 2, 3])
                for kc in range(8):
                    P.op("pe", (lambda e, bk=bk, slot=slot, kc=kc, ts=ts: e.matmul(P.ps[bk][:, :], lhsT=hT[:, kc, 128 * ts:128 * ts + 128],
                                                                                   rhs=slot[:, kc * 512:kc * 512 + 512],
                                                                                   start=(kc == 0), stop=(kc == 7))),
                         reads=wn + [("hT", kc)], writes=[("ps", bk)])
                vb = vtok[vi % 2]
                vn = "vtok%d" % (vi % 2)
                P.op("act", (lambda e, vb=vb, bk=bk: e.activation(out=vb[:], in_=P.ps[bk][:, :], func=AF.Copy)),
                     reads=[("ps", bk)], writes=[vn])
                P.op("sp", (lambda e, vb=vb, ts=ts, hf=hf, m0=m0: e.dma_start(
                    out=Vo[m0 + 128 * ts:m0 + 128 * ts + 128, 512 * hf:512 * hf + 512], in_=vb[:])),
                     reads=[vn], dma="v%d" % (vi % 2))
                vi += 1
        P.ring_done(2)
    assert dbg < 99 or P.cons == len(P.blocks)
    P.S.emit()
    return nc


NQT = SEQ // 512


def build_B(nheads=4, nqt=NQT):
    P = Prog()
    nc = P.nc
    QTh = P.dram("QTh", [4, 64, SEQ], BF16, "ExternalInput")
    KTh = P.dram("KTh", [4, 64, SEQ], BF16, "ExternalInput")
    Vh = P.dram("Vh", [4, SEQ, 64], BF16, "ExternalInput")
    OTh = P.dram("OTh", [4, 64, SEQ], BF16, "ExternalOutput")
    I32 = mybir.dt.int32

    KA = [P.sb("KA%d" % i, [128, SEQ], BF16) for i in range(2)]
    VA = [P.sb("VA%d" % i, [128, 128, 128], BF16) for i in range(2)]
    QA = [P.sb("QA%d" % i, [128, 512], BF16) for i in range(4)]
    kms = P.sb("kms", [64, 64], F32)
    kmT = [P.sb("kmT%d" % i, [64, 64], BF16) for i in range(2)]
    gsb = [P.sb("gsb%d" % i, [128, 64], F32) for i in range(4)]
    m8 = [P.sb("m8_%d" % i, [128, 8], F32) for i in range(4)]
    negp = [P.sb("negp%d" % i, [128, 128], BF16) for i in range(4)]
    ident = P.sb("ident", [128, 128], BF16)
    TRI = [P.sb("TRI%d" % i, [128, 512], BF16) for i in range(4)]
    PT = [P.sb("PT%d" % i, [128, 512], BF16) for i in range(4)]
    rden = [P.sb("rden%d" % i, [64, 512], F32) for i in range(2)]
    OTb = [P.sb("OTb%d" % i, [64, 512], BF16) for i in range(2)]

    for i in range(2):
        P.op("pool", (lambda e, i=i: e.iota(KA[i][64:128, :], [[1, 64], [0, 256]], base=0, channel_multiplier=-1,
                                            allow_small_or_imprecise_dtypes=True)), writes=[("KAE", i)])
        P.op("dve", (lambda e, i=i: e.tensor_single_scalar(out=KA[i][64:128, :], in_=KA[i][64:128, :], scalar=0.0, op=ALU.is_equal)),
             reads=[("KAE", i)], writes=[("KAE", i)])
        P.op("dve", (lambda e, i=i: e.memset(VA[i][:, :, 64:128], 1.0)), writes=[("VA1", i)])
    P.op("pool", (lambda e: e.iota(ident[:], [[1, 128]], base=0, channel_multiplier=-1, allow_small_or_imprecise_dtypes=True)),
         writes=["ident"])
    P.op("dve", (lambda e: e.tensor_single_scalar(out=ident[:], in_=ident[:], scalar=0.0, op=ALU.is_equal)),
         reads=["ident"], writes=["ident"])
    for kk in range(4):
        P.op("pool", (lambda e, kk=kk: e.iota(TRI[kk][:], [[1, 512]], base=-128 * kk, channel_multiplier=-1,
                                              allow_small_or_imprecise_dtypes=True)), writes=[("TRI", kk)])
        P.op("dve", (lambda e, kk=kk: e.tensor_scalar(out=TRI[kk][:], in0=TRI[kk][:], scalar1=0.0, scalar2=NEG,
                                                      op0=ALU.is_lt, op1=ALU.mult)), reads=[("TRI", kk)], writes=[("TRI", kk)])
    for i in range(4):
        P.op("dve", (lambda e, i=i: e.memset(negp[i][:], 0.0)), writes=[("negp", i)])

    S_BANKS = [0, 1, 2, 3]
    O_BANKS = [4, 5]
    G_BANK = 6
    T_BANK = 7

    def load_head(h):
        i = h % 2
        P.op("sp", (lambda e: e.dma_start(out=KA[i][0:64, :], in_=KTh[h, :, :])), writes=[("KAK", i)], dma="ka%d" % i)
        vsrc = Vh[h, :, :].rearrange("(kt p) d -> p kt d", p=128)
        for q in range(4):
            P.op("sp", (lambda e, q=q: e.dma_start(out=VA[i][:, vch * q:vch * q + vch, 0:64], in_=vsrc[:, vch * q:vch * q + vch, :])),
                 writes=[("VAV", i, q)], dma="va%d_%d" % (i, q))

    def head_prep(h):
        i = h % 2
        kview = KA[i][0:64, :].rearrange("p (b k) -> p b k", k=256)
        P.op("dve", (lambda e: e.tensor_reduce(out=kms[:], in_=kview, axis=mybir.AxisListType.X, op=ALU.add)),
             reads=[("KAK", i)], writes=["kms"])
        P.op("dve", (lambda e: e.tensor_scalar(out=kmT[i][:], in0=kms[:], scalar1=1.0 / 256.0, scalar2=None, op0=ALU.mult)),
             reads=["kms"], writes=[("kmT", i)])
        for s in range(4):
            P.op("dve", (lambda e, s=s: e.memset(gsb[s][:], -1e30)), writes=[("gsb", s)])

    items = [(h, t) for h in range(nheads) for t in range(nqt)]
    qa_of = {}

    def qload(n):
        if n >= len(items):
            return
        h, t = items[n]
        qi = n % 4
        qa = QA[qi]
        qa_of[n] = qi
        P.op("sp", (lambda e: e.dma_start(out=qa[0:64, :], in_=QTh[h, :, 512 * t:512 * t + 512])),
             writes=[("QAq", qi)], dma="qa%d" % qi)

    def gate1(n):
        h, t = items[n]
        i = h % 2
        qi = qa_of[n]
        qa = QA[qi]
        if t == 0:
            head_prep(h)
        for s in range(4):
            ob = 2 * t + s // 2
            P.op("pe", (lambda e, s=s: e.matmul(P.ps[G_BANK][:, 64 * s:64 * s + 64], lhsT=qa[0:64, 128 * s:128 * s + 128],
                                                 rhs=kmT[i][:, :], start=True, stop=True)),
                 reads=[("QAq", qi), ("kmT", i)], writes=[("ps", G_BANK)])
        for s in range(4):
            ob = 2 * t + s // 2
            if ob > 0:
                P.op("dve", (lambda e, s=s, ob=ob: e.tensor_copy(out=gsb[s][:, 0:ob], in_=P.ps[G_BANK][:, 64 * s:64 * s + ob])),
                     reads=[("ps", G_BANK)], writes=[("gsb", s)])
            P.op("dve", (lambda e, s=s, ob=ob: e.memset(gsb[s][:, ob:ob + 1], 1e30)), writes=[("gsb", s)])
            P.op("dve", (lambda e, s=s: e.max(out=m8[s][:], in_=gsb[s][:, :])), reads=[("gsb", s)], writes=[("m8", s)])
            P.op("dve", (lambda e, s=s: e.tensor_scalar(out=negp[s][:, 64:128], in0=gsb[s][:, :], scalar1=m8[s][:, 3:4], scalar2=NEG,
                                                        op0=ALU.is_lt, op1=ALU.mult)),
                 reads=[("gsb", s), ("m8", s)], writes=[("negp", s)])

    def gate2(n):
        qi = qa_of[n]
        qa = QA[qi]
        for s in range(4):
            P.op("pe", (lambda e, s=s: e.matmul(P.ps[T_BANK][:, 128 * s:128 * s + 128], lhsT=negp[s][:, :], rhs=ident[:, :],
                                                 start=True, stop=True)),
                 reads=[("negp", s), "ident"], writes=[("ps", T_BANK)])
        P.op("dve", (lambda e: e.tensor_copy(out=qa[64:128, :], in_=P.ps[T_BANK][64:128, :])),
             reads=[("ps", T_BANK)], writes=[("QAn", qi)])

    pt_i = [0]

    def main(n):
        h, t = items[n]
        i = h % 2
        qi = qa_of[n]
        qa = QA[qi]
        nk = 4 * t + 4
        obk = O_BANKS[n % 2]
        live = {}
        for step in range(nk + 2):
            if step < nk:
                kt = step
                sb_ = P.bank("S", S_BANKS)
                diag = kt >= 4 * t
                P.op("pe", (lambda e, kt=kt, sb_=sb_, diag=diag: e.matmul(P.ps[sb_][:, :], lhsT=KA[i][:, 128 * kt:128 * kt + 128], rhs=qa[:, :],
                                                                          start=True, stop=(not diag))),
                     reads=[("KAK", i), ("KAE", i), ("QAq", qi), ("QAn", qi)], writes=[("ps", sb_)])
                if diag:
                    P.op("pe", (lambda e, kt=kt, sb_=sb_: e.matmul(P.ps[sb_][:, :], lhsT=ident[:, :], rhs=TRI[kt - 4 * t][:, :],
                                                                   start=False, stop=True)),
                         reads=["ident", ("TRI", kt - 4 * t)], writes=[("ps", sb_)])
                pi = pt_i[0] % 4
                pt_i[0] += 1
                P.op("act", (lambda e, pi=pi, sb_=sb_: e.activation(out=PT[pi][:, :], in_=P.ps[sb_][:, :], func=AF.Exp)),
                     reads=[("ps", sb_)], writes=[("PT", pi)])
                live[kt] = pi
            if step >= 2:
                kt = step - 2
                pi = live.pop(kt)
                P.op("pe", (lambda e, kt=kt, pi=pi: e.matmul(P.ps[obk][:, :], lhsT=VA[i][:, kt, :], rhs=PT[pi][:, :],
                                                              start=(kt == 0), stop=(kt == nk - 1))),
                     reads=[("VAV", i, kt // vch), ("VA1", i), ("PT", pi)], writes=[("ps", obk)])
            if step == 0:
                qload(n + 2)
            if step == 1 and n + 1 < len(items):
                gate1(n + 1)
            if step == max(nk - 1, 2) and n + 1 < len(items):
                gate2(n + 1)
        r = n % 2
        P.op("dve", (lambda e: e.reciprocal(out=rden[r][:, :], in_=P.ps[obk][64:128, :])), reads=[("ps", obk)], writes=[("rden", r)])
        P.op("dve", (lambda e: e.tensor_tensor(out=OTb[r][:, :], in0=P.ps[obk][0:64, :], in1=rden[r][:, :], op=ALU.mult)),
             reads=[("ps", obk), ("rden", r)], writes=[("OTb", r)])
        P.op("sp", (lambda e: e.dma_start(out=OTh[h, :, 512 * t:512 * t + 512], in_=OTb[r][:, :])), reads=[("OTb", r)], dma="ot%d" % r)

    load_head(0)
    qload(0)
    qload(1)
    gate1(0)
    gate2(0)
    for n in range(len(items)):
        h, t = items[n]
        if t == 0 and h + 1 < nheads:
            load_head(h + 1)
        main(n)
    P.S.emit()
    return nc


def build_C():
    P = Prog()
    nc = P.nc
    x1T = P.dram("x1T", [128, 8, NT], F32, "ExternalInput")
    OTt = P.dram("OTt", [128, 8, NT], BF16, "ExternalInput")
    gains_d = P.dram("gains", [128, 2, 8], F32, "ExternalInput")
    wcf_d = P.dram("wcf", [128, 3, 44], F32, "ExternalInput")
    w_o = P.dram("w_o", [D, D], F32, "ExternalInput")
    w_up = P.dram("w_up", [D, 2 * DFF], F32, "ExternalInput")
    w_down = P.dram("w_down", [DFF, D], F32, "ExternalInput")
    outT = P.dram("outT", [128, 8, TOK], F32, "ExternalOutput")

    P.ring_init()
    P.consts()
    P.ffn_setup()
    gains = P.sb("gains_sb", [128, 2, 8], F32)
    wcf = P.sb("wcf_sb", [128, 3, 44], F32)
    P.op("sp", lambda e: e.dma_start(out=gains[:], in_=gains_d[:, :, :]), writes=["gains"], dma="c0")
    P.op("sp", lambda e: e.dma_start(out=wcf[:], in_=wcf_d[:, :, :]), writes=["wc"], dma="c2")
    xts = [P.sb("xt%d" % i, [128, 8, 512], F32) for i in range(2)]
    ots = [P.sb("ot%d" % i, [128, 8, 512], BF16) for i in range(2)]
    yo = [P.sb("yo%d" % i, [128, 8, 512], F32) for i in range(2)]
    hT = P.sb("hT", [128, 8, 512], BF16)
    for ti, (off, W) in enumerate(TILES):
        P.ring_plan(Prog.sq_blocks(w_o, "wo") + Prog.ffn_blocks(w_up, w_down))
    for ti, (off, W) in enumerate(TILES):
        xt = xts[ti % 2]
        xn = "xt%d" % (ti % 2)
        ot = ots[ti % 2]
        on = "ot%d" % (ti % 2)
        P.op("sp", (lambda e, xt=xt, off=off, W=W: e.dma_start(out=xt[:, :, 0:W], in_=x1T[:, :, off:off + W])),
             writes=[(xn, c) for c in range(8)], dma="xin%d" % (ti % 2))
        P.op("sp", (lambda e, ot=ot, off=off, W=W: e.dma_start(out=ot[:, :, 0:W], in_=OTt[:, :, off:off + W])),
             writes=[(on, c) for c in range(8)], dma="oin%d" % (ti % 2))
        P.proj_add(xt, xn, ot, on, W, "wo", [6, 7])
        P.rmsnorm_stats(xt, xn, W, [0, 1, 2, 3])
        P.rmsnorm_apply(xt, xn, W, gains[:, 0, :], hT, "hT")
        P.ffn(xt, xn, hT, W, wcf)
        if ti == 0:
            continue
        m0 = off - HALO
        y = yo[ti % 2]
        yn = "yo%d" % (ti % 2)
        P.rmsnorm_stats(xt, xn, W, [0, 1, 2, 3])
        P.rmsnorm_apply(xt, xn, W, gains[:, 1, :], y, yn)
        P.op("sp", (lambda e, y=y, m0=m0: e.dma_start(out=outT[:, :, m0:m0 + 512], in_=y[:, :, :])),
             reads=[(yn, c) for c in range(8)], dma="yout%d" % (ti % 2))
    assert P.cons == len(P.blocks)
    P.S.emit()
    return nc


NTL = SEQ // 512
OWN0 = 24
NSL = 9


OWN_TILES = [12, 13, 14, 15, 28, 29, 30, 31]


def build_fused(n_tiles=NTL, n_heads=NH, own_tiles=None):
    own_tiles = sorted(OWN_TILES if own_tiles is None else own_tiles)
    halo_tiles = sorted(T - 1 for T in own_tiles if T - 1 not in own_tiles)
    assert all(T >= 0 for T in halo_tiles)
    slot_of = {T: i for i, T in enumerate(sorted(set(own_tiles) | set(halo_tiles)))}
    out_of = {T: i for i, T in enumerate(own_tiles)}
    P = Prog()
    P.use_wscr = True
    nc = P.nc
    nsl = len(slot_of)
    L = n_tiles * 512
    xT = P.dram("xT", [128, 8, L], F32, "ExternalInput")
    pos = P.dram("pos", [1, L], F32, "ExternalInput")
    gbias_d = P.dram("gbias", [128, 64], F32, "ExternalInput")
    P.bar_src = pos[0:1, 0:64]
    gains_d = P.dram("gains", [128, 5, 8], F32, "ExternalInput")
    wcm_d = P.dram("wcm", [128, 3, 8], F32, "ExternalInput")
    wcf0_d = P.dram("wcf0", [128, 3, 44], F32, "ExternalInput")
    wcf1_d = P.dram("wcf1", [128, 3, 44], F32, "ExternalInput")
    w_in = P.dram("w_in", [D, 3 * D], F32, "ExternalInput")
    w_out = P.dram("w_out", [D, D], F32, "ExternalInput")
    w_up0 = P.dram("w_up0", [D, 2 * DFF], F32, "ExternalInput")
    w_down0 = P.dram("w_down0", [DFF, D], F32, "ExternalInput")
    w_qkv = P.dram("w_qkv", [D, 3 * D], F32, "ExternalInput")
    w_o = P.dram("w_o", [D, D], F32, "ExternalInput")
    w_up1 = P.dram("w_up1", [D, 2 * DFF], F32, "ExternalInput")
    w_down1 = P.dram("w_down1", [DFF, D], F32, "ExternalInput")
    outT = P.dram("outT", [128, 8, TOK], F32, "ExternalOutput")
    x1s = P.dram("x1s", [128, 8, nsl * 512], F32, "Internal")
    KTs = P.dram("KTs", [D, L], BF16, "Internal")
    Vs = P.dram("Vs", [L, D], BF16, "Internal")
    QTs = P.dram("QTs", [D, nsl * 512], BF16, "Internal")
    OTs = P.dram("OTs", [D, nsl * 512], BF16, "Internal")
    half = HD // 2
    inv = (np.float32(10000.0) ** (-(np.arange(half, dtype=np.float32) / np.float32(half)))).astype(np.float32)
    TWO_PI = 2.0 * np.pi
    C1 = float(np.float32(6.28125))
    C2 = float(np.float32(TWO_PI - 6.28125))

    P.phase_begin()
    P.ring_init()
    P.consts()
    P.ffn_setup()
    gains = P.sb("gains_sb", [128, 5, 8], F32)
    wcm = P.sb("wcm_sb", [128, 3, 8], F32)
    wcf = P.sb("wcf_sb", [128, 3, 44], F32)
    P.op("sp", lambda e: e.dma_start(out=gains[:], in_=gains_d[:, :, :]), writes=["gains"], dma="c0")
    P.op("sp", lambda e: e.dma_start(out=wcm[:], in_=wcm_d[:, :, :]), writes=["wc"], dma="c1")
    P.op("sp", lambda e: e.dma_start(out=wcf[:], in_=wcf0_d[:, :, :]), writes=["wc"], dma="c2")
    xts = [P.sb("xt%d" % i, [128, 8, 512], F32) for i in range(2)]
    hT = P.sb("hT", [128, 8, 512], BF16)
    hTm = P.sb("hTm", [128, 8, 512], BF16)
    sqn = [P.sb("sqn%d" % i, [128, 512], BF16) for i in range(8)]
    yT = P.sb("yT", [128, 8, 512], BF16)
    cbuf = [P.sb("cbuf%d" % i, [128, 512], F32) for i in range(2)]
    cvb = [P.sb("cvb%d" % i, [128, 514], F32) for i in range(2)]
    tm = [P.sb("tm%d" % i, [128, 512], F32) for i in range(2)]
    hsm = P.sb("hsm", [128, 8, 2], F32)
    P.op("dve", lambda e: e.memset(hsm[:], 0.0), writes=[("hsm", i) for i in range(8)])
    inv_row = P.sb("inv_row", [1, 128], F32)
    for i in range(half):
        P.op("dve", (lambda e, i=i: e.memset(inv_row[0:1, i:128:32], float(inv[i]))), writes=["inv_row"])
    sgn = P.sb("sgn", [128, 1], F32)
    for q in range(4):
        P.op("dve", (lambda e, q=q: e.memset(sgn[32 * q:32 * q + 32, :], -1.0 if q % 2 == 0 else 1.0)), writes=["sgn"])
    pos_sbs = [P.sb("pos_sb%d" % i, [1, 512], F32) for i in range(2)]
    ang = P.sb("ang", [128, 512], F32)
    kf = P.sb("kf", [128, 512], F32)
    ki = P.sb("ki", [128, 512], mybir.dt.int32)
    cosT = P.sb("cosT", [128, 512], F32)
    sinT = P.sb("sinT", [128, 512], F32)
    rt1 = [P.sb("rt1_%d" % i, [128, 512], F32) for i in range(2)]
    rxs = [P.sb("rxs_%d" % i, [128, 512], F32) for i in range(2)]
    rxc = [P.sb("rxc_%d" % i, [128, 512], F32) for i in range(2)]
    NOB = 3
    qkb = [P.sb("qkb%d" % i, [128, 512], BF16) for i in range(NOB)]
    vtok = [P.sb("vtok%d" % i, [128, 512], BF16) for i in range(NOB)]

    def mixer_blocks():
        bl = []
        for j in range(4):
            segs = []
            for si in range(3):
                src = w_in[:, si * D + 256 * j: si * D + 256 * j + 256].rearrange("(k p) n -> p k n", p=128)
                segs.append((256 * si, 8, 256, 768, src))
            bl.append(("in%d" % j, segs))
        return bl

    def qkv_blocks(with_q):
        bl = []
        for j in (range(6) if with_q else range(2, 6)):
            src = w_qkv[:, 512 * j:512 * j + 512].rearrange("(k p) n -> p k n", p=128)
            bl.append(("qkv%d" % j, [(0, 8, 512, 512, src)]))
        return bl

    for ti in range(n_tiles):
        P.ring_plan(mixer_blocks() + Prog.sq_blocks(w_out, "wo") + Prog.ffn_blocks(w_up0, w_down0) + qkv_blocks(ti in slot_of))
    rope_i = [0]
    tri_i = [0]
    W = 512
    for ti in range(n_tiles):
        off = ti * 512
        if ti == 16:
            P.mid_barrier()
        xt = xts[ti % 2]
        xn = "xt%d" % (ti % 2)

        def xload(tj):
            if tj >= n_tiles:
                return
            P.op("sp", (lambda e: e.dma_start(out=xts[tj % 2][:, :, :], in_=xT[:, :, tj * 512:tj * 512 + 512])),
                 writes=[("xt%d" % (tj % 2), c) for c in range(8)], dma="xin%d" % (tj % 2))
        if ti == 0:
            xload(0)
        xload(ti + 1)

        def norm1_sq(tj):
            if tj >= n_tiles:
                return
            for c in range(8):
                P.op("act", (lambda e, c=c: e.activation(out=sqn[c][:, :], in_=xts[tj % 2][:, c, :], func=AF.Square)),
                     reads=[("xt%d" % (tj % 2), c)], writes=[("sqn", c)])

        def norm1_rest(tj):
            if tj >= n_tiles:
                return
            bk_ = P.bank("A", [0, 1, 2, 3])
            for c in range(8):
                P.op("pe", (lambda e, c=c, bk_=bk_: e.matmul(P.ps[bk_][:, :], lhsT=P.ones[:], rhs=sqn[c][:, :], start=(c == 0), stop=(c == 7))),
                     reads=[("sqn", c), "ones"], writes=[("ps", bk_)])
            P.op("dve", (lambda e, bk_=bk_: e.tensor_scalar(out=P.rsb[:, :], in0=P.ps[bk_][:, :], scalar1=1.0 / D, scalar2=EPS, op0=ALU.mult, op1=ALU.add)),
                 reads=[("ps", bk_)], writes=["rsb"])
            P.op("act", (lambda e: e.activation(out=P.rsb[:, :], in_=P.rsb[:, :], func=AF.Sqrt)), reads=["rsb"], writes=["rsb"])
            P.op("dve", (lambda e: e.reciprocal(out=P.rstd[:, :], in_=P.rsb[:, :])), reads=["rsb"], writes=["rstd"])
            P.rmsnorm_apply(xts[tj % 2], "xt%d" % (tj % 2), W, gains[:, 0, :], hTm, "hTm")
        if ti == 0:
            norm1_sq(0)
            norm1_rest(0)
        for j in range(4):
            slot, wn = P.ring_next("in%d" % j)
            for s_ in range(2):
                cj = 2 * j + s_
                pi = tri_i[0] % 2
                tri_i[0] += 1
                bks = [P.bank("A", [0, 1, 2, 3, 4, 5]) for _ in range(3)]
                for si in (1, 2, 0):
                    bk = bks[si]
                    for kc in range(8):
                        lw = slot[:, kc * 768 + si * 256 + s_ * 128: kc * 768 + si * 256 + s_ * 128 + 128]
                        P.op("pe", (lambda e, bk=bk, lw=lw, kc=kc: e.matmul(P.ps[bk][:, :], lhsT=lw, rhs=hTm[:, kc, :],
                                                                            start=(kc == 0), stop=(kc == 7))),
                             reads=wn + [("hTm", kc)], writes=[("ps", bk)])
                cb, cv, t = cbuf[pi], cvb[pi], tm[pi]
                P.op("act", (lambda e, cb=cb, bk=bks[1]: e.activation(out=cb[:, :], in_=P.ps[bk][:, :], func=AF.Copy)),
                     reads=[("ps", bks[1])], writes=["cbuf%d" % pi])
                P.op("dve", (lambda e, cb=cb, cv=cv, bk=bks[2]: e.tensor_tensor(out=cv[:, 2:514], in0=cb[:, :], in1=P.ps[bk][:, :], op=ALU.mult)),
                     reads=["cbuf%d" % pi, ("ps", bks[2])], writes=[("cvb%d" % pi, "m")])
                P.conv3(cv, "cvb%d" % pi, t, "tm%d" % pi, W, wcm, cj, hsm, "hsm")
                P.op("dve", (lambda e, t=t, cj=cj, bk=bks[0]: e.tensor_tensor(out=yT[:, cj, :], in0=t[:, :], in1=P.ps[bk][:, :], op=ALU.mult)),
                     reads=["tm%d" % pi, ("ps", bks[0])], writes=[("yT", cj)])
            P.ring_done()
        P.proj_add(xt, xn, yT, "yT", W, "wo", [6, 7])
        steps = []
        pos_sb = pos_sbs[ti % 2]

        def S_(eng, fn, **kw):
            steps.append(lambda eng=eng, fn=fn, kw=kw: P.op(eng, fn, **kw))
        S_("sp", (lambda e, pos_sb=pos_sb, off=off: e.dma_start(out=pos_sb[:], in_=pos[:, off:off + 512])), writes=[("pos", ti % 2)], dma="pos%d" % (ti % 2))
        S_("pe", (lambda e, pos_sb=pos_sb: e.matmul(P.ps[7][:, :], lhsT=inv_row[:], rhs=pos_sb[0:1, :], start=True, stop=True)),
           reads=["inv_row", ("pos", ti % 2)], writes=[("ps", 7)])
        S_("dve", (lambda e: e.tensor_copy(out=ang[:], in_=P.ps[7][:, :])), reads=[("ps", 7)], writes=["ang"])
        for which, dst in (("sin", sinT), ("cos", cosT)):
            if which == "cos":
                S_("dve", (lambda e: e.tensor_scalar(out=ang[:], in0=ang[:], scalar1=float(np.pi / 2), scalar2=None, op0=ALU.add)),
                   reads=["ang"], writes=["ang"])
            S_("dve", (lambda e: e.tensor_scalar(out=ki[:], in0=ang[:], scalar1=float(1.0 / TWO_PI), scalar2=None, op0=ALU.mult)),
               reads=["ang"], writes=["ki"])
            S_("dve", (lambda e: e.tensor_copy(out=kf[:], in_=ki[:])), reads=["ki"], writes=["kf"])
            S_("dve", (lambda e, dst=dst: e.scalar_tensor_tensor(out=dst[:], in0=kf[:], scalar=-C1, in1=ang[:], op0=ALU.mult, op1=ALU.add)),
               reads=["kf", "ang"], writes=[which])
            S_("dve", (lambda e, dst=dst: e.scalar_tensor_tensor(out=dst[:], in0=kf[:], scalar=-C2, in1=dst[:], op0=ALU.mult, op1=ALU.add)),
               reads=["kf", which], writes=[which])
            S_("dve", (lambda e, dst=dst: e.tensor_scalar(out=kf[:], in0=dst[:], scalar1=float(np.pi), scalar2=-TWO_PI, op0=ALU.is_gt, op1=ALU.mult)),
               reads=[which], writes=["kf"])
            S_("dve", (lambda e, dst=dst: e.tensor_tensor(out=dst[:], in0=dst[:], in1=kf[:], op=ALU.add)), reads=[which, "kf"], writes=[which])
            S_("dve", (lambda e, dst=dst: e.tensor_scalar(out=kf[:], in0=dst[:], scalar1=float(-np.pi), scalar2=TWO_PI, op0=ALU.is_lt, op1=ALU.mult)),
               reads=[which], writes=["kf"])
            S_("dve", (lambda e, dst=dst: e.tensor_tensor(out=dst[:], in0=dst[:], in1=kf[:], op=ALU.add)), reads=[which, "kf"], writes=[which])
            if which == "sin":
                S_("act", (lambda e: e.activation(out=sinT[:], in_=sinT[:], func=AF.Sin, scale=sgn[:, 0:1])), reads=["sin", "sgn"], writes=["sin"])
            else:
                S_("act", (lambda e: e.activation(out=cosT[:], in_=cosT[:], func=AF.Sin)), reads=["cos"], writes=["cos"])
        P.rmsnorm_stats(xt, xn, W, [0, 1, 2, 3])
        P.rmsnorm_apply(xt, xn, W, gains[:, 1, :], hT, "hT")
        P.ffn(xt, xn, hT, W, wcf, hook=steps)
        with_q = ti in slot_of
        if with_q:
            so = slot_of[ti] * 512
            P.op("sp", (lambda e, xt=xt, so=so: e.dma_start(out=x1s[:, :, so:so + 512], in_=xt[:, :, :])),
                 reads=[(xn, c) for c in range(8)], dma="xout%d" % (ti % 2))
        P.rmsnorm_stats(xt, xn, W, [0, 1, 2, 3])
        P.rmsnorm_apply(xt, xn, W, gains[:, 2, :], hT, "hT")
        for j in (range(4) if with_q else range(2, 4)):
            slot, wn = P.ring_next("qkv%d" % j)
            scale = 0.125 if j < 2 else 1.0
            for o in range(4):
                ch = (j % 2) * 4 + o
                ri = rope_i[0] % 2
                oi = rope_i[0] % NOB
                rope_i[0] += 1
                bk = P.bank("A", [0, 1, 2, 3])
                for kc in range(8):
                    lw = slot[:, kc * 512 + o * 128: kc * 512 + o * 128 + 128]
                    P.op("pe", (lambda e, bk=bk, lw=lw, kc=kc: e.matmul(P.ps[bk][:, :], lhsT=lw, rhs=hT[:, kc, :], start=(kc == 0), stop=(kc == 7))),
                         reads=wn + [("hT", kc)], writes=[("ps", bk)])
                t1, xs, xc, ob = rt1[ri], rxs[ri], rxc[ri], qkb[oi]
                P.op("act", (lambda e, xc=xc, bk=bk, scale=scale: e.activation(out=xc[:, :], in_=P.ps[bk][:, :], func=AF.Copy, scale=scale)),
                     reads=[("ps", bk)], writes=["rxc%d" % ri])
                for q in range(4):
                    src = 32 * (q ^ 1)
                    P.op("act", (lambda e, xs=xs, q=q, src=src, bk=bk, scale=scale: e.activation(
                        out=xs[32 * q:32 * q + 32, :], in_=P.ps[bk][src:src + 32, :], func=AF.Copy, scale=scale)),
                         reads=[("ps", bk)], writes=[("rxs%d" % ri, q)])
                P.op("dve", (lambda e, t1=t1, xc=xc: e.tensor_tensor(out=t1[:], in0=xc[:], in1=cosT[:], op=ALU.mult)),
                     reads=["rxc%d" % ri, "cos"], writes=["rt1%d" % ri])
                P.op("dve", (lambda e, xs=xs: e.tensor_tensor(out=xs[:], in0=xs[:], in1=sinT[:], op=ALU.mult)),
                     reads=[("rxs%d" % ri, q) for q in range(4)] + ["sin"], writes=[("rxs%d" % ri, q) for q in range(4)])
                P.op("dve", (lambda e, t1=t1, xs=xs, ob=ob: e.tensor_tensor(out=ob[:], in0=t1[:], in1=xs[:], op=ALU.add)),
                     reads=["rt1%d" % ri] + [("rxs%d" % ri, q) for q in range(4)], writes=["qkb%d" % oi])
                if j < 2:
                    so = slot_of[ti] * 512
                    P.op("sp", (lambda e, ob=ob, ch=ch, so=so: e.dma_start(out=QTs[128 * ch:128 * ch + 128, so:so + 512], in_=ob[:])),
                         reads=["qkb%d" % oi], dma="qk%d" % oi)
                else:
                    P.op("sp", (lambda e, ob=ob, ch=ch, off=off: e.dma_start(out=KTs[128 * ch:128 * ch + 128, off:off + 512], in_=ob[:])),
                         reads=["qkb%d" % oi], dma="qk%d" % oi)
            P.ring_done()
        vi = 0
        slots_v = [P.ring_next("qkv4"), P.ring_next("qkv5")]
        for ts in range(4):
            if ts == 1:
                norm1_sq(ti + 1)
            if ts == 3:
                norm1_rest(ti + 1)
            for hf in range(2):
                slot, wn = slots_v[hf]
                bk = P.bank("A", [0, 1, 2, 3])
                for kc in range(8):
                    P.op("pe", (lambda e, bk=bk, slot=slot, kc=kc, ts=ts: e.matmul(P.ps[bk][:, :], lhsT=hT[:, kc, 128 * ts:128 * ts + 128],
                                                                                   rhs=slot[:, kc * 512:kc * 512 + 512], start=(kc == 0), stop=(kc == 7))),
                         reads=wn + [("hT", kc)], writes=[("ps", bk)])
                vb = vtok[vi % NOB]
                vn = "vtok%d" % (vi % NOB)
                P.op("act", (lambda e, vb=vb, bk=bk: e.activation(out=vb[:], in_=P.ps[bk][:, :], func=AF.Copy)), reads=[("ps", bk)], writes=[vn])
                P.op("sp", (lambda e, vb=vb, ts=ts, hf=hf, off=off: e.dma_start(
                    out=Vs[off + 128 * ts:off + 128 * ts + 128, 512 * hf:512 * hf + 512], in_=vb[:])), reads=[vn], dma="v%d" % (vi % NOB))
                vi += 1
        P.ring_done(2)
    assert P.cons == len(P.blocks)
    P.phase_end()

    P.phase_begin()
    KA = [P.sb("KA%d" % i, [128, L], BF16) for i in range(2)]
    VA = [P.sb("VA%d" % i, [128, L // 128, 128], BF16) for i in range(2)]
    QA = [P.sb("QA%d" % i, [128, 512], BF16) for i in range(4)]
    kms = P.sb("kms", [64, 64], F32)
    kmT = [P.sb("kmT%d" % i, [64, 64], BF16) for i in range(2)]
    gsb = [P.sb("gsb%d" % i, [128, 64], F32) for i in range(4)]
    m8 = [P.sb("m8_%d" % i, [128, 8], F32) for i in range(4)]
    negp = [P.sb("negp%d" % i, [128, 128], BF16) for i in range(4)]
    ident = P.sb("ident", [128, 128], BF16)
    TRI = [P.sb("TRI%d" % i, [128, 512], BF16) for i in range(4)]
    PT = [P.sb("PT%d" % i, [128, 512], BF16) for i in range(4)]
    rden = [P.sb("rden%d" % i, [64, 512], F32) for i in range(2)]
    OTb = [P.sb("OTb%d" % i, [64, 512], BF16) for i in range(2)]
    gbias = P.sb("gbias_sb", [128, 64], F32)
    P.op("sp", lambda e: e.dma_start(out=gbias[:], in_=gbias_d[:, :]), writes=["gbias"], dma="c0")
    nblk = L // 256
    for i in range(2):
        P.op("pool", (lambda e, i=i: e.iota(KA[i][64:128, :], [[1, nblk], [0, 256]], base=0, channel_multiplier=-1,
                                            allow_small_or_imprecise_dtypes=True)), writes=[("KAE", i)])
        P.op("dve", (lambda e, i=i: e.tensor_single_scalar(out=KA[i][64:128, :], in_=KA[i][64:128, :], scalar=0.0, op=ALU.is_equal)),
             reads=[("KAE", i)], writes=[("KAE", i)])
        P.op("dve", (lambda e, i=i: e.memset(VA[i][:, :, 64:128], 1.0)), writes=[("VA1", i)])
    P.op("pool", (lambda e: e.iota(ident[:], [[1, 128]], base=0, channel_multiplier=-1, allow_small_or_imprecise_dtypes=True)), writes=["ident"])
    P.op("dve", (lambda e: e.tensor_single_scalar(out=ident[:], in_=ident[:], scalar=0.0, op=ALU.is_equal)), reads=["ident"], writes=["ident"])
    for kk in range(4):
        P.op("pool", (lambda e, kk=kk: e.iota(TRI[kk][:], [[1, 512]], base=-128 * kk, channel_multiplier=-1,
                                              allow_small_or_imprecise_dtypes=True)), writes=[("TRI", kk)])
        P.op("dve", (lambda e, kk=kk: e.tensor_scalar(out=TRI[kk][:], in0=TRI[kk][:], scalar1=0.0, scalar2=NEG, op0=ALU.is_lt, op1=ALU.mult)),
             reads=[("TRI", kk)], writes=[("TRI", kk)])
    for i in range(4):
        P.op("dve", (lambda e, i=i: e.memset(negp[i][:], 0.0)), writes=[("negp", i)])
    S_BANKS = [0, 1, 2, 3]
    O_BANKS = [4, 5]
    G_BANK = 6
    T_BANK = 7
    nkt_all = L // 128
    vch = min(32, nkt_all)
    nvq = nkt_all // vch

    def load_head(h):
        i = h % 2
        P.op("sp", (lambda e: e.dma_start(out=KA[i][0:64, :], in_=KTs[64 * h:64 * h + 64, :])), writes=[("KAK", i)], dma="ka%d" % i)
        vsrc = Vs[:, 64 * h:64 * h + 64].rearrange("(kt p) d -> p kt d", p=128)
        for q in range(nvq):
            P.op("sp", (lambda e, q=q: e.dma_start(out=VA[i][:, vch * q:vch * q + vch, 0:64], in_=vsrc[:, vch * q:vch * q + vch, :])),
                 writes=[("VAV", i, q)], dma="va%d_%d" % (i, q))

    def head_prep(h):
        i = h % 2
        kview = KA[i][0:64, :].rearrange("p (b k) -> p b k", k=256)
        P.op("dve", (lambda e: e.tensor_reduce(out=kms[:, 0:nblk], in_=kview, axis=mybir.AxisListType.X, op=ALU.add)),
             reads=[("KAK", i)], writes=["kms"])
        P.op("dve", (lambda e: e.tensor_scalar(out=kmT[i][:, 0:nblk], in0=kms[:, 0:nblk], scalar1=1.0 / 256.0, scalar2=None, op0=ALU.mult)),
             reads=["kms"], writes=[("kmT", i)])
        for s_ in range(4):
            P.op("dve", (lambda e, s_=s_: e.memset(gsb[s_][:], -1e30)), writes=[("gsb", s_)])

    items = []
    for h in range(n_heads):
        for T in sorted(slot_of):
            if T in out_of:
                items.append((h, T, 0, 512, slot_of[T] * 512, [(128 * s_, 128, 2 * T + s_ // 2) for s_ in range(4)]))
            else:
                items.append((h, T, 512 - HALO, HALO, slot_of[T] * 512, [(0, HALO, 2 * T + 1)]))
    qa_of = {}

    def qload(n):
        if n >= len(items):
            return
        h, T, c0, Wq, sc, subs = items[n]
        qi = n % 4
        qa_of[n] = qi
        P.op("sp", (lambda e: e.dma_start(out=QA[qi][0:64, 0:Wq], in_=QTs[64 * h:64 * h + 64, sc + c0:sc + c0 + Wq])),
             writes=[("QAq", qi)], dma="qa%d" % qi)

    def gate1(n):
        h, T, c0, Wq, sc, subs = items[n]
        i = h % 2
        qi = qa_of[n]
        qa = QA[qi]
        if n == 0 or items[n - 1][0] != h:
            head_prep(h)
        for s_, (o_, nq, ob) in enumerate(subs):
            P.op("pe", (lambda e, s_=s_, o_=o_, nq=nq: e.matmul(P.ps[G_BANK][0:nq, 64 * s_:64 * s_ + nblk], lhsT=qa[0:64, o_:o_ + nq],
                                                                 rhs=kmT[i][:, 0:nblk], start=True, stop=True)),
                 reads=[("QAq", qi), ("kmT", i)], writes=[("ps", G_BANK)])
        for s_, (o_, nq, ob) in enumerate(subs):
            P.op("dve", (lambda e, s_=s_, nq=nq, ob=ob: e.tensor_tensor(out=gsb[s_][0:nq, 0:ob], in0=P.ps[G_BANK][0:nq, 64 * s_:64 * s_ + ob],
                                                                         in1=gbias[0:nq, 0:ob], op=ALU.add)),
                 reads=[("ps", G_BANK), "gbias"], writes=[("gsb", s_)])
            P.op("dve", (lambda e, s_=s_, nq=nq, ob=ob: e.memset(gsb[s_][0:nq, ob:ob + 1], 1e30)), writes=[("gsb", s_)])
            P.op("dve", (lambda e, s_=s_, nq=nq: e.max(out=m8[s_][0:nq, :], in_=gsb[s_][0:nq, :])), reads=[("gsb", s_)], writes=[("m8", s_)])
            P.op("dve", (lambda e, s_=s_, nq=nq: e.tensor_scalar(out=negp[s_][0:nq, 64:128], in0=gsb[s_][0:nq, :], scalar1=m8[s_][0:nq, 3:4],
                                                                  scalar2=NEG, op0=ALU.is_lt, op1=ALU.mult)),
                 reads=[("gsb", s_), ("m8", s_)], writes=[("negp", s_)])

    def gate2(n):
        h, T, c0, Wq, sc, subs = items[n]
        qi = qa_of[n]
        qa = QA[qi]
        for s_, (o_, nq, ob) in enumerate(subs):
            P.op("pe", (lambda e, s_=s_, o_=o_, nq=nq: e.matmul(P.ps[T_BANK][:, o_:o_ + nq], lhsT=negp[s_][0:nq, :], rhs=ident[0:nq, 0:nq],
                                                                 start=True, stop=True)),
                 reads=[("negp", s_), "ident"], writes=[("ps", T_BANK)])
        P.op("dve", (lambda e: e.tensor_copy(out=qa[64:128, 0:Wq], in_=P.ps[T_BANK][64:128, 0:Wq])), reads=[("ps", T_BANK)], writes=[("QAn", qi)])

    pt_i = [0]

    def main(n):
        h, T, c0, Wq, sc, subs = items[n]
        i = h % 2
        qi = qa_of[n]
        qa = QA[qi]
        nk = 4 * T + 4
        obk = O_BANKS[n % 2]
        live = {}
        for step in range(nk + 2):
            if step < nk:
                kt = step
                sb_ = P.bank("S", S_BANKS)
                diag = kt >= 4 * T
                P.op("pe", (lambda e, kt=kt, sb_=sb_, diag=diag: e.matmul(P.ps[sb_][:, 0:Wq], lhsT=KA[i][:, 128 * kt:128 * kt + 128], rhs=qa[:, 0:Wq],
                                                                          start=True, stop=(not diag))),
                     reads=[("KAK", i), ("KAE", i), ("QAq", qi), ("QAn", qi)], writes=[("ps", sb_)])
                if diag:
                    P.op("pe", (lambda e, kt=kt, sb_=sb_: e.matmul(P.ps[sb_][:, 0:Wq], lhsT=ident[:, :], rhs=TRI[kt - 4 * T][:, c0:c0 + Wq],
                                                                   start=False, stop=True)),
                         reads=["ident", ("TRI", kt - 4 * T)], writes=[("ps", sb_)])
                pi = pt_i[0] % 4
                pt_i[0] += 1
                P.op("act", (lambda e, pi=pi, sb_=sb_: e.activation(out=PT[pi][:, 0:Wq], in_=P.ps[sb_][:, 0:Wq], func=AF.Exp)),
                     reads=[("ps", sb_)], writes=[("PT", pi)])
                live[kt] = pi
            if step >= 2:
                kt = step - 2
                pi = live.pop(kt)
                P.op("pe", (lambda e, kt=kt, pi=pi: e.matmul(P.ps[obk][:, 0:Wq], lhsT=VA[i][:, kt, :], rhs=PT[pi][:, 0:Wq],
                                                              start=(kt == 0), stop=(kt == nk - 1))),
                     reads=[("VAV", i, kt // vch), ("VA1", i), ("PT", pi)], writes=[("ps", obk)])
            if step == 0:
                qload(n + 2)
            if step == 1 and n + 1 < len(items):
                gate1(n + 1)
            if step == max(nk - 1, 2) and n + 1 < len(items):
                gate2(n + 1)
        r = n % 2
        P.op("dve", (lambda e: e.reciprocal(out=rden[r][:, 0:Wq], in_=P.ps[obk][64:128, 0:Wq])), reads=[("ps", obk)], writes=[("rden", r)])
        P.op("dve", (lambda e: e.tensor_tensor(out=OTb[r][:, 0:Wq], in0=P.ps[obk][0:64, 0:Wq], in1=rden[r][:, 0:Wq], op=ALU.mult)),
             reads=[("ps", obk), ("rden", r)], writes=[("OTb", r)])
        P.op("sp", (lambda e: e.dma_start(out=OTs[64 * h:64 * h + 64, sc + c0:sc + c0 + Wq], in_=OTb[r][:, 0:Wq])), reads=[("OTb", r)], dma="ot%d" % r)

    load_head(0)
    qload(0)
    qload(1)
    gate1(0)
    gate2(0)
    for n in range(len(items)):
        h = items[n][0]
        if (n == 0 or items[n - 1][0] != h) and h == 8:
            P.mid_barrier()
        if (n == 0 or items[n - 1][0] != h) and h + 1 < n_heads:
            load_head(h + 1)
        main(n)
    P.phase_end()

    P.phase_begin()
    P.ring_init()
    P.consts()
    P.ffn_setup()
    gains = P.sb("gains_sb", [128, 5, 8], F32)
    wcf = P.sb("wcf_sb", [128, 3, 44], F32)
    P.op("sp", lambda e: e.dma_start(out=gains[:], in_=gains_d[:, :, :]), writes=["gains"], dma="c0")
    P.op("sp", lambda e: e.dma_start(out=wcf[:], in_=wcf1_d[:, :, :]), writes=["wc"], dma="c2")
    xts = [P.sb("xt%d" % i, [128, 8, 512], F32) for i in range(2)]
    ots = [P.sb("ot%d" % i, [128, 8, 512], BF16) for i in range(2)]
    yo = [P.sb("yo%d" % i, [128, 8, 512], F32) for i in range(2)]
    hT = P.sb("hT", [128, 8, 512], BF16)
    ctiles = []
    for T in sorted(slot_of):
        if T in out_of:
            ctiles.append((slot_of[T] * 512, 512))
        else:
            ctiles.append((slot_of[T] * 512 + 512 - HALO, HALO))
    cout = [out_of.get(T) for T in sorted(slot_of)]
    OTv = OTs.rearrange("(c p) t -> p c t", p=128)
    for ti, (off, W) in enumerate(ctiles):
        P.ring_plan(Prog.sq_blocks(w_o, "wo") + Prog.ffn_blocks(w_up1, w_down1))
    for ti, (off, W) in enumerate(ctiles):
        xt = xts[ti % 2]
        xn = "xt%d" % (ti % 2)
        ot = ots[ti % 2]
        on = "ot%d" % (ti % 2)
        def cload(tj):
            if tj >= len(ctiles):
                return
            off_, W_ = ctiles[tj]
            P.op("sp", (lambda e: e.dma_start(out=xts[tj % 2][:, :, 0:W_], in_=x1s[:, :, off_:off_ + W_])),
                 writes=[("xt%d" % (tj % 2), c) for c in range(8)], dma="xin%d" % (tj % 2))
            P.op("sp", (lambda e: e.dma_start(out=ots[tj % 2][:, :, 0:W_], in_=OTv[:, :, off_:off_ + W_])),
                 writes=[("ot%d" % (tj % 2), c) for c in range(8)], dma="oin%d" % (tj % 2))
        if ti == 0:
            cload(0)
        cload(ti + 1)
        P.proj_add(xt, xn, ot, on, W, "wo", [6, 7])
        P.rmsnorm_stats(xt, xn, W, [0, 1, 2, 3])
        P.rmsnorm_apply(xt, xn, W, gains[:, 3, :], hT, "hT")
        P.ffn(xt, xn, hT, W, wcf)
        if cout[ti] is None:
            continue
        m0 = cout[ti] * 512
        y = yo[ti % 2]
        yn = "yo%d" % (ti % 2)
        P.rmsnorm_stats(xt, xn, W, [0, 1, 2, 3])
        P.rmsnorm_apply(xt, xn, W, gains[:, 4, :], y, yn)
        P.op("sp", (lambda e, y=y, m0=m0: e.dma_start(out=outT[:, :, m0:m0 + 512], in_=y[:, :, :])),
             reads=[(yn, c) for c in range(8)], dma="yout%d" % (ti % 2))
    assert P.cons == len(P.blocks)
    P.phase_end(last=True)
    return nc


def _lay_cols(v, n):
    return np.ascontiguousarray(v.reshape(n, 128).T)


def _fm(a):
    n = a.shape[0]
    return np.ascontiguousarray(a.T.reshape(8, 128, n).transpose(1, 0, 2))


_CACHE = {}


def _prog(name, fn):
    if name not in _CACHE:
        _CACHE[name] = fn()
    return _CACHE[name]


def kernel_unfused(x, mix_norm, sc_w_in, sc_w_conv, sc_w_out, moba_w_qkv, moba_w_o,
           ffn_norm, ffn_w_up, ffn_w_conv, ffn_w_down, final_norm):
    f32 = np.float32
    x = np.asarray(x, f32)
    cores = list(range(NCORE))
    gainsA = np.ascontiguousarray(np.stack([_lay_cols(np.asarray(mix_norm[0], f32), 8), _lay_cols(np.asarray(ffn_norm[0], f32), 8),
                                            _lay_cols(np.asarray(mix_norm[1], f32), 8)], axis=1))
    wcm = np.ascontiguousarray(np.stack([_lay_cols(np.asarray(sc_w_conv[0][j], f32), 8) for j in range(3)], axis=1))
    wcf0 = np.ascontiguousarray(np.stack([_lay_cols(np.asarray(ffn_w_conv[0][j], f32), 44) for j in range(3)], axis=1))
    wcf1 = np.ascontiguousarray(np.stack([_lay_cols(np.asarray(ffn_w_conv[1][j], f32), 44) for j in range(3)], axis=1))
    in_maps = []
    for c in cores:
        b, i = divmod(c, 4)
        xs = np.zeros((NT, D), f32)
        lo = i * TOK - HALO
        if lo < 0:
            xs[HALO:] = x[b, 0:TOK]
        else:
            xs[:] = x[b, lo:lo + NT]
        in_maps.append(dict(xT=_fm(xs), pos=(i * TOK + np.arange(TOK, dtype=f32))[None, :], gains=gainsA, wcm=wcm, wcf=wcf0,
                            w_in=np.asarray(sc_w_in[0], f32), w_out=np.asarray(sc_w_out[0], f32),
                            w_up=np.asarray(ffn_w_up[0], f32), w_down=np.asarray(ffn_w_down[0], f32),
                            w_qkv=np.asarray(moba_w_qkv[0], f32)))
    resA = run_bass_kernel_spmd(_prog("A", build_A), in_maps, core_ids=cores).results
    in_maps = []
    for c in cores:
        b, g = divmod(c, 4)
        rows = slice(256 * g, 256 * g + 256)
        QTh = np.concatenate([resA[4 * b + i]["QT"][rows] for i in range(4)], axis=1).reshape(4, 64, SEQ)
        KTh = np.concatenate([resA[4 * b + i]["KT"][rows] for i in range(4)], axis=1).reshape(4, 64, SEQ)
        Vb = np.concatenate([resA[4 * b + i]["V"][:, rows] for i in range(4)], axis=0)
        Vh = np.ascontiguousarray(Vb.reshape(SEQ, 4, 64).transpose(1, 0, 2))
        in_maps.append(dict(QTh=np.ascontiguousarray(QTh), KTh=np.ascontiguousarray(KTh), Vh=Vh))
    resB = run_bass_kernel_spmd(_prog("B", build_B), in_maps, core_ids=cores).results
    gainsC = np.ascontiguousarray(np.stack([_lay_cols(np.asarray(ffn_norm[1], f32), 8), _lay_cols(np.asarray(final_norm, f32), 8)], axis=1))
    in_maps = []
    for c in cores:
        b, i = divmod(c, 4)
        OTb = np.concatenate([resB[4 * b + g]["OTh"].reshape(256, SEQ) for g in range(4)], axis=0)
        ot = np.zeros((D, NT), OTb.dtype)
        lo = i * TOK - HALO
        if lo < 0:
            ot[:, HALO:] = OTb[:, 0:TOK]
        else:
            ot[:] = OTb[:, lo:lo + NT]
        OTt = np.ascontiguousarray(ot.reshape(8, 128, NT).transpose(1, 0, 2))
        in_maps.append(dict(x1T=resA[c]["x1T"], OTt=OTt, gains=gainsC, wcf=wcf1, w_o=np.asarray(moba_w_o[0], f32),
                            w_up=np.asarray(ffn_w_up[1], f32), w_down=np.asarray(ffn_w_down[1], f32)))
    resC = run_bass_kernel_spmd(_prog("C", build_C), in_maps, core_ids=cores).results
    out = np.empty((BATCH, SEQ, D), f32)
    for c in cores:
        b, i = divmod(c, 4)
        out[b, i * TOK:(i + 1) * TOK] = resC[c]["outT"].transpose(1, 0, 2).reshape(D, TOK).T
    return out


def kernel(x, mix_norm, sc_w_in, sc_w_conv, sc_w_out, moba_w_qkv, moba_w_o,
           ffn_norm, ffn_w_up, ffn_w_conv, ffn_w_down, final_norm):
    f32 = np.float32
    x = np.asarray(x, f32)
    cores = list(range(NCORE))
    lay = _lay_cols
    gains = np.ascontiguousarray(np.stack([lay(np.asarray(mix_norm[0], f32), 8), lay(np.asarray(ffn_norm[0], f32), 8),
                                           lay(np.asarray(mix_norm[1], f32), 8), lay(np.asarray(ffn_norm[1], f32), 8),
                                           lay(np.asarray(final_norm, f32), 8)], axis=1))
    wcm = np.ascontiguousarray(np.stack([lay(np.asarray(sc_w_conv[0][j], f32), 8) for j in range(3)], axis=1))
    wcf0 = np.ascontiguousarray(np.stack([lay(np.asarray(ffn_w_conv[0][j], f32), 44) for j in range(3)], axis=1))
    wcf1 = np.ascontiguousarray(np.stack([lay(np.asarray(ffn_w_conv[1][j], f32), 44) for j in range(3)], axis=1))
    shared = dict(gains=gains, wcm=wcm, wcf0=wcf0, wcf1=wcf1,
                  w_in=np.asarray(sc_w_in[0], f32), w_out=np.asarray(sc_w_out[0], f32),
                  w_up0=np.asarray(ffn_w_up[0], f32), w_down0=np.asarray(ffn_w_down[0], f32),
                  w_qkv=np.asarray(moba_w_qkv[0], f32), w_o=np.asarray(moba_w_o[0], f32),
                  w_up1=np.asarray(ffn_w_up[1], f32), w_down1=np.asarray(ffn_w_down[1], f32))
    SEG = 2048
    in_maps = []
    for c in cores:
        b, j = divmod(c, 4)
        npad = (3 - j) * SEG
        nreal = SEQ - npad
        xs = np.zeros((SEQ, D), f32)
        xs[npad:] = x[b, 0:nreal]
        pos = np.zeros((1, SEQ), f32)
        pos[0, npad:] = np.arange(nreal, dtype=f32)
        gb = np.zeros((128, 64), f32)
        gb[:, :npad // 256] = -2e30
        m = dict(shared)
        m.update(xT=_fm(xs), pos=pos, gbias=gb)
        in_maps.append(m)
    res = run_bass_kernel_spmd(_prog("F", build_fused), in_maps, core_ids=cores).results
    out = np.empty((BATCH, SEQ, D), f32)
    for c in cores:
        b, j = divmod(c, 4)
        o = res[c]["outT"].transpose(1, 0, 2).reshape(D, TOK).T
        out[b, j * SEG:(j + 1) * SEG] = o[0:SEG]
        out[b, (4 + j) * SEG:(5 + j) * SEG] = o[SEG:2 * SEG]
    return out
```
